# Optimizing a Trainium2 kernel written in Bass

```python
import math
import jax, jax.numpy as jnp
from jax import lax
import numpy as np

D_MODEL = 2048
BATCH = 4
SEQ = 2048
DEPTH = 4

MLA_HEADS = 8
MLA_NOPE = 128
MLA_ROPE = 64
MLA_V = 128
MLA_Q_LORA = 512
MLA_KV_LORA = 256
MLA_WIDTH = MLA_HEADS * MLA_V
ROPE_THETA = 10000.0

SSM_WIDTH = D_MODEL // 4
SSM_GROUP = 16
SSM_GROUPS = SSM_WIDTH // SSM_GROUP
SSM_STATE = 64

DIL_WIDTH = D_MODEL // 4
DIL_HEAD_DIM = 64
DIL_HEADS = DIL_WIDTH // DIL_HEAD_DIM
DIL_PATTERNS = ((128, 1), (512, 4), (2048, 16))

BLOCK = 128
MIX_WIDTH = MLA_WIDTH + SSM_WIDTH + DIL_WIDTH
IN_SPLITS = (MLA_Q_LORA, MLA_KV_LORA, MLA_ROPE, SSM_WIDTH, DIL_WIDTH, DIL_WIDTH, DIL_WIDTH)
IN_WIDTH = sum(IN_SPLITS)
D_FF = ((8 * D_MODEL + 3 * 256 - 1) // (3 * 256)) * 256
NORM_EPS = 1e-6

kernel_name = "hymba_mla_s5_dilated_hybrid"


def rms_norm(x, g):
    xf = x.astype(jnp.float32)
    y = xf * lax.rsqrt(jnp.mean(xf * xf, axis=-1, keepdims=True) + NORM_EPS)
    return (y * g.astype(jnp.float32)).astype(x.dtype)


def apply_rope(x, pos):
    half = x.shape[-1] // 2
    inv_freq = ROPE_THETA ** (-jnp.arange(half, dtype=jnp.float32) / half)
    ang = pos.astype(jnp.float32)[:, None] * inv_freq[None, :]
    cos = jnp.cos(ang)[None, :, None, :]
    sin = jnp.sin(ang)[None, :, None, :]
    xf = x.astype(jnp.float32)
    x1, x2 = xf[..., :half], xf[..., half:]
    return jnp.concatenate([x1 * cos - x2 * sin, x2 * cos + x1 * sin], axis=-1).astype(x.dtype)


def mla_mixer(c_q, c_kv, k_rope, g_q, w_uq, g_kv, w_ukv):
    B, S, _ = c_q.shape
    pos = jnp.arange(S)
    q = (rms_norm(c_q, g_q) @ w_uq).reshape(B, S, MLA_HEADS, MLA_NOPE + MLA_ROPE)
    q_nope = q[..., :MLA_NOPE]
    q_pe = apply_rope(q[..., MLA_NOPE:], pos)
    kv = (rms_norm(c_kv, g_kv) @ w_ukv).reshape(B, S, MLA_HEADS, MLA_NOPE + MLA_V)
    k_nope, v = kv[..., :MLA_NOPE], kv[..., MLA_NOPE:]
    k_pe = apply_rope(k_rope[:, :, None, :], pos)[:, :, 0]
    scale = (MLA_NOPE + MLA_ROPE) ** -0.5
    nb = S // BLOCK
    qn_b = q_nope.reshape(B, nb, BLOCK, MLA_HEADS, MLA_NOPE).transpose(1, 0, 2, 3, 4)
    qp_b = q_pe.reshape(B, nb, BLOCK, MLA_HEADS, MLA_ROPE).transpose(1, 0, 2, 3, 4)
    kpos = jnp.arange(S)

    def one_block(args):
        b, qn, qp = args
        s = (jnp.einsum('bqhd,bkhd->bhqk', qn, k_nope).astype(jnp.float32)
             + jnp.einsum('bqhd,bkd->bhqk', qp, k_pe).astype(jnp.float32)) * scale
        qpos = b * BLOCK + jnp.arange(BLOCK)
        causal = qpos[:, None] >= kpos[None, :]
        s = jnp.where(causal[None, None], s, -jnp.inf)
        p = jax.nn.softmax(s, axis=-1).astype(v.dtype)
        return jnp.einsum('bhqk,bkhd->bqhd', p, v)

    out = lax.map(one_block, (jnp.arange(nb), qn_b, qp_b))
    return out.transpose(1, 0, 2, 3, 4).reshape(B, S, MLA_WIDTH)


def s5_mixer(u, a_re, a_im, b_re, b_im, c_re, c_im, d_skip, log_dt, w_glu, b_glu):
    B, S, _ = u.shape
    f32 = jnp.float32
    uf = u.astype(f32).reshape(B, S, SSM_GROUPS, SSM_GROUP)
    lam = lax.complex(jnp.minimum(a_re.astype(f32), -1e-4), a_im.astype(f32))
    dt = jnp.exp(log_dt.astype(f32))[:, None]
    a_bar = jnp.exp(lam * dt)
    b_cplx = lax.complex(b_re.astype(f32), b_im.astype(f32))
    b_bar = ((a_bar - 1.0) / lam)[..., None] * b_cplx
    bu = jnp.einsum('gnp,bsgp->bsgn', b_bar, uf.astype(jnp.complex64))
    a_seq = jnp.broadcast_to(a_bar, bu.shape)

    def combine(left, right):
        a_l, h_l = left
        a_r, h_r = right
        return a_r * a_l, a_r * h_l + h_r

    _, h = lax.associative_scan(combine, (a_seq, bu), axis=1)
    c_cplx = lax.complex(c_re.astype(f32), c_im.astype(f32))
    y = jnp.einsum('gpn,bsgn->bsgp', c_cplx, h).real + d_skip.astype(f32) * uf
    y = jax.nn.gelu(y.reshape(B, S, SSM_WIDTH))
    z = y @ w_glu.astype(f32) + b_glu.astype(f32)
    out = z[..., :SSM_WIDTH] * jax.nn.sigmoid(z[..., SSM_WIDTH:])
    return out.astype(u.dtype)


def strided_fold(x, dil):
    B, S = x.shape[:2]
    rest = x.shape[2:]
    return x.reshape(B, S // dil, dil, *rest).swapaxes(1, 2).reshape(B * dil, S // dil, *rest)


def strided_unfold(x, batch, dil):
    L = x.shape[1]
    rest = x.shape[2:]
    return x.reshape(batch, dil, L, *rest).swapaxes(1, 2).reshape(batch, L * dil, *rest)


def banded_window_attention(q, k, v, span):
    Z, L, H, D = q.shape
    nb = -(-L // BLOCK)
    Lp = nb * BLOCK
    pad = ((0, 0), (0, Lp - L), (0, 0), (0, 0))
    qb, kb, vb = [jnp.pad(t, pad).reshape(Z, nb, BLOCK, H, D) for t in (q, k, v)]

    def with_prev(t):
        prev = jnp.pad(t, ((0, 0), (1, 0), (0, 0), (0, 0), (0, 0)))[:, :-1]
        return jnp.concatenate([prev, t], axis=2)

    kk, vv = with_prev(kb), with_prev(vb)
    s = jnp.einsum('znqhd,znkhd->znhqk', qb, kk).astype(jnp.float32) * (D ** -0.5)
    qpos = jnp.arange(nb)[:, None] * BLOCK + jnp.arange(BLOCK)[None, :]
    kpos = (jnp.arange(nb)[:, None] - 1) * BLOCK + jnp.arange(2 * BLOCK)[None, :]
    dist = qpos[:, :, None] - kpos[:, None, :]
    mask = (dist >= 0) & (dist <= span) & (kpos[:, None, :] >= 0)
    s = jnp.where(mask[None, :, None], s, -jnp.inf)
    m = jnp.max(s, axis=-1, keepdims=True)
    p = jnp.exp(s - m)
    l = jnp.sum(p, axis=-1, keepdims=True)
    o = jnp.einsum('znhqk,znkhd->znqhd', (p / l).astype(v.dtype), vv)
    lse = (m + jnp.log(l))[..., 0].transpose(0, 1, 3, 2).reshape(Z, Lp, H)
    return o.reshape(Z, Lp, H, D)[:, :L], lse[:, :L]


def dilated_mixer(qd, kd, vd):
    B, S, _ = qd.shape
    q, k, v = [t.reshape(B, S, DIL_HEADS, DIL_HEAD_DIM) for t in (qd, kd, vd)]
    outs, lses = [], []
    for window, dil in DIL_PATTERNS:
        o, lse = banded_window_attention(strided_fold(q, dil), strided_fold(k, dil),
                                         strided_fold(v, dil), window // dil)
        outs.append(strided_unfold(o, B, dil).astype(jnp.float32))
        lses.append(strided_unfold(lse, B, dil))
    wts = jax.nn.softmax(jnp.stack(lses, axis=0), axis=0)
    out = wts[0][..., None] * outs[0] + wts[1][..., None] * outs[1] + wts[2][..., None] * outs[2]
    return out.reshape(B, S, DIL_WIDTH).astype(qd.dtype)


def setup_inputs(seed: int = 0) -> dict:
    key = jax.random.key(seed)
    ks = jax.random.split(key, 32)
    L = DEPTH
    nrm = lambda k, shape, scale: jax.random.normal(k, shape, jnp.float32) * scale
    gain = lambda k, n: 1.0 + 0.02 * jax.random.normal(k, (L, n), jnp.float32)
    out_scale = (2 * DEPTH) ** -0.5
    n_idx = jnp.arange(SSM_STATE, dtype=jnp.float32)
    return {
        "x": jax.random.normal(ks[0], (BATCH, SEQ, D_MODEL), jnp.float32),
        "g_mix": gain(ks[1], D_MODEL),
        "w_in": nrm(ks[2], (L, D_MODEL, IN_WIDTH), D_MODEL ** -0.5),
        "g_q": gain(ks[3], MLA_Q_LORA),
        "w_uq": nrm(ks[4], (L, MLA_Q_LORA, MLA_HEADS * (MLA_NOPE + MLA_ROPE)), MLA_Q_LORA ** -0.5),
        "g_kv": gain(ks[5], MLA_KV_LORA),
        "w_ukv": nrm(ks[6], (L, MLA_KV_LORA, MLA_HEADS * (MLA_NOPE + MLA_V)), MLA_KV_LORA ** -0.5),
        "a_re": -0.5 + nrm(ks[7], (L, SSM_GROUPS, SSM_STATE), 0.01),
        "a_im": math.pi * n_idx + nrm(ks[8], (L, SSM_GROUPS, SSM_STATE), 0.01),
        "b_re": nrm(ks[9], (L, SSM_GROUPS, SSM_STATE, SSM_GROUP), (2 * SSM_GROUP) ** -0.5),
        "b_im": nrm(ks[10], (L, SSM_GROUPS, SSM_STATE, SSM_GROUP), (2 * SSM_GROUP) ** -0.5),
        "c_re": nrm(ks[11], (L, SSM_GROUPS, SSM_GROUP, SSM_STATE), 0.5),
        "c_im": nrm(ks[12], (L, SSM_GROUPS, SSM_GROUP, SSM_STATE), 0.5),
        "d_skip": nrm(ks[13], (L, SSM_GROUPS, SSM_GROUP), 1.0),
        "log_dt": jax.random.uniform(ks[14], (L, SSM_GROUPS), jnp.float32,
                                     math.log(1e-3), math.log(1e-1)),
        "w_glu": nrm(ks[15], (L, SSM_WIDTH, 2 * SSM_WIDTH), SSM_WIDTH ** -0.5),
        "b_glu": nrm(ks[16], (L, 2 * SSM_WIDTH), 0.01),
        "g_out_mla": gain(ks[17], MLA_WIDTH),
        "g_out_ssm": gain(ks[18], SSM_WIDTH),
        "g_out_dil": gain(ks[19], DIL_WIDTH),
        "w_o": nrm(ks[20], (L, MIX_WIDTH, D_MODEL), MIX_WIDTH ** -0.5 * out_scale),
        "g_ffn": gain(ks[21], D_MODEL),
        "w_gate": nrm(ks[22], (L, D_MODEL, D_FF), D_MODEL ** -0.5),
        "w_up": nrm(ks[23], (L, D_MODEL, D_FF), D_MODEL ** -0.5),
        "w_down": nrm(ks[24], (L, D_FF, D_MODEL), D_FF ** -0.5 * out_scale),
        "g_final": 1.0 + 0.02 * jax.random.normal(ks[25], (D_MODEL,), jnp.float32),
    }


def reference(x, g_mix, w_in, g_q, w_uq, g_kv, w_ukv, a_re, a_im, b_re, b_im, c_re, c_im,
              d_skip, log_dt, w_glu, b_glu, g_out_mla, g_out_ssm, g_out_dil, w_o,
              g_ffn, w_gate, w_up, w_down, g_final):
    split_at = np.cumsum(IN_SPLITS)[:-1].tolist()
    for l in range(DEPTH):
        h = rms_norm(x, g_mix[l])
        proj = h @ w_in[l]
        c_q, c_kv, k_rope, u, qd, kd, vd = jnp.split(proj, split_at, axis=-1)
        y_mla = mla_mixer(c_q, c_kv, k_rope, g_q[l], w_uq[l], g_kv[l], w_ukv[l])
        y_ssm = s5_mixer(u, a_re[l], a_im[l], b_re[l], b_im[l], c_re[l], c_im[l],
                         d_skip[l], log_dt[l], w_glu[l], b_glu[l])
        y_dil = dilated_mixer(qd, kd, vd)
        y = jnp.concatenate([rms_norm(y_mla, g_out_mla[l]),
                             rms_norm(y_ssm, g_out_ssm[l]),
                             rms_norm(y_dil, g_out_dil[l])], axis=-1)
        x = x + y @ w_o[l]
        h = rms_norm(x, g_ffn[l])
        x = x + (jax.nn.silu(h @ w_gate[l]) * (h @ w_up[l])) @ w_down[l]
    return rms_norm(x, g_final)
```

```python
import numpy as np
import concourse.bass as bass
import concourse.mybir as mybir
from concourse.bass_utils import run_bass_kernel_spmd

F32 = mybir.dt.float32
BF16 = mybir.dt.bfloat16
F32R = mybir.dt.float32r
FP32R_STATS = False
AF = mybir.ActivationFunctionType
ALU = mybir.AluOpType
AX = mybir.AxisListType

D = 2048
S = 2048
NB = 4
DEPTH = 4
IN_W = 2880
DFF = 5632
EPS = 1e-6
NCORES = 8


class DSem:
    def __init__(self, sem):
        self.sem = sem
        self.cnt = 0


class Buf:
    def __init__(self, t, name):
        self.t = t
        self.name = name
        self.w = None
        self.r = {}
        self.dsem = None

    def __getitem__(self, idx):
        return View(self, self.t[idx])

    def ap(self):
        return View(self, self.t.ap() if hasattr(self.t, "ap") else self.t[:])


class View:
    def __init__(self, buf, ap):
        self.buf = buf
        self.ap = ap

    def __getitem__(self, idx):
        return View(self.buf, self.ap[idx])

    def bcast(self, shape):
        return View(self.buf, self.ap.to_broadcast(shape))

    def rr(self, pat, **kw):
        return View(self.buf, self.ap.rearrange(pat, **kw))


class Eng:
    def __init__(self, name, obj, sem):
        self.name = name
        self.obj = obj
        self.sem = sem
        self.cnt = 0
        self.waited = {}


class Ctx:
    def __init__(self):
        self.nc = bass.Bass("TRN2", target_bir_lowering=False)
        nc = self.nc
        self.engs = {}
        for n in ["tensor", "vector", "scalar", "gpsimd", "sync"]:
            self.engs[n] = Eng(n, getattr(nc, n), nc.alloc_semaphore("sem_" + n))
        self.uid = 0
        self.stack = None
        self.free_dsems = []
        self.all_dsems = []
        self.phase_bufs = []
        self.csem = None
        self.ccnt = 0

    def _nm(self, name):
        self.uid += 1
        return f"{name}_{self.uid}"

    def _reg(self, b):
        if self.stack is not None:
            self.phase_bufs.append(b)
        return b

    def sb(self, name, shape, dtype, n=1):
        nm = self._nm(name)
        if self.stack is not None:
            t = self.stack.enter_context(self.nc.sbuf_tensor(nm, list(shape), dtype))
        else:
            t = self.nc.alloc_sbuf_tensor(nm, list(shape), dtype)
        if n == 1:
            return self._reg(Buf(t, nm))
        return [self._reg(Buf(t, f"{nm}_{i}")) for i in range(n)]

    def sbt(self, name, shape, dtype):
        nm = self._nm(name)
        if self.stack is not None:
            return self.stack.enter_context(self.nc.sbuf_tensor(nm, list(shape), dtype))
        return self.nc.alloc_sbuf_tensor(nm, list(shape), dtype)

    def mkbuf(self, t, name):
        return self._reg(Buf(t, self._nm(name)))

    def ps(self, name, shape, dtype=F32):
        nm = self._nm(name)
        if self.stack is not None:
            t = self.stack.enter_context(self.nc.psum_tensor(nm, list(shape), dtype))
        else:
            t = self.nc.alloc_psum_tensor(nm, list(shape), dtype)
        return self._reg(Buf(t, nm))

    def dram(self, name, shape, dtype, kind=None):
        if kind is None:
            t = self.nc.dram_tensor(name, list(shape), dtype)
        else:
            t = self.nc.dram_tensor(name, list(shape), dtype, kind=kind)
        return Buf(t, name)

    def begin_phase(self):
        from contextlib import ExitStack
        assert self.stack is None
        self.stack = ExitStack()
        self.phase_bufs = []
        self.scopes = []

    def push_scope(self):
        from contextlib import ExitStack
        self.scopes.append((self.stack, self.phase_bufs))
        self.stack = ExitStack()
        self.phase_bufs = []

    def pop_scope(self):
        self.barrier()
        for b in self.phase_bufs:
            if b.dsem is not None:
                self.free_dsems.append(b.dsem)
                b.dsem = None
        self.stack.close()
        self.stack, self.phase_bufs = self.scopes.pop()

    def end_phase(self):
        self.barrier()
        for b in self.phase_bufs:
            if b.dsem is not None:
                self.free_dsems.append(b.dsem)
                b.dsem = None
        self.phase_bufs = []
        self.stack.close()
        self.stack = None

    def _get_dsem(self, b):
        if b.dsem is None:
            if self.free_dsems:
                b.dsem = self.free_dsems.pop()
            else:
                b.dsem = DSem(self.nc.alloc_semaphore(self._nm("dsem")))
                self.all_dsems.append(b.dsem)
        return b.dsem

    def barrier(self):
        for eng in self.engs.values():
            for other in self.engs.values():
                if other is not eng and other.cnt > 0:
                    self._wait(eng, ("e", other, other.cnt))
            for ds in self.all_dsems:
                if ds.cnt > 0:
                    self._wait(eng, ("d", ds))
            if self.ccnt > 0:
                self._wait(eng, ("c", self.ccnt))

    def _wait(self, eng, tok):
        kind = tok[0]
        if kind == "e":
            _, src, val = tok
            if src is eng and eng.name == "tensor":
                return
            sem = src.sem
        elif kind == "d":
            ds = tok[1]
            sem = ds.sem
            val = ds.cnt * 16
        else:
            sem = self.csem
            val = tok[1]
        key = id(sem)
        if eng.waited.get(key, 0) >= val:
            return
        eng.waited[key] = val
        eng.obj.wait_ge(sem, val)

    def _deps(self, eng, w, r):
        toks = []
        for b in r:
            if b.w is not None:
                toks.append(b.w)
        for b in w:
            if b.w is not None:
                toks.append(b.w)
            toks.extend(b.r.values())
        for tok in toks:
            self._wait(eng, tok)

    def emit(self, engname, fn, w, r):
        eng = self.engs[engname]
        w = [v.buf if isinstance(v, View) else v for v in w]
        r = [v.buf if isinstance(v, View) else v for v in r if isinstance(v, (View, Buf))]
        self._deps(eng, w, r)
        inst = fn(eng.obj)
        eng.cnt += 1
        inst.then_inc(eng.sem, 1)
        tok = ("e", eng, eng.cnt)
        for b in w:
            b.w = tok
            b.r = {}
        for b in r:
            if b.w is not tok:
                b.r[eng.name] = tok
        return tok

    def dma(self, out, in_, q="sync", **kw):
        eng = self.engs[q]
        ob, ib = out.buf, in_.buf
        self._deps(eng, [ob], [ib])
        ds = self._get_dsem(ob)
        eng.obj.dma_start(out=out.ap, in_=in_.ap, **kw).then_inc(ds.sem, 16)
        ds.cnt += 1
        tok = ("d", ds)
        ob.w = tok
        ob.r = {}
        ib.r[id(ds)] = tok
        return tok

    def allgather(self, dst, src):
        eng = self.engs["gpsimd"]
        self._deps(eng, [dst], [src])
        if self.csem is None:
            self.csem = self.nc.alloc_semaphore("csem")
        eng.obj.collective_compute("AllGather", ALU.bypass, replica_groups=[[0, 1], [2, 3], [4, 5], [6, 7]],
                                   ins=[src.t.ap().opt()], outs=[dst.t.ap().opt()]).then_inc(self.csem)
        self.ccnt += 1
        tok = ("c", self.ccnt)
        dst.w = tok
        dst.r = {}
        src.r["cc"] = tok
        return tok

    def finish(self, out_bufs):
        eng = self.engs["sync"]
        for b in out_bufs:
            if b.w is not None:
                self._wait(eng, b.w)

    def mm(self, out, lhsT, rhs, start=True, stop=True, **kw):
        return self.emit("tensor", lambda e: e.matmul(out.ap, lhsT=lhsT.ap, rhs=rhs.ap, start=start, stop=stop, **kw),
                         [out], [lhsT, rhs])

    def transpose(self, out, in_, ident):
        return self.emit("tensor", lambda e: e.transpose(out.ap, in_.ap, ident.ap), [out], [in_, ident])

    def act(self, out, in_, func, scale=None, bias=None, eng="scalar"):
        kw = {}
        rd = [in_]
        if scale is not None:
            if isinstance(scale, View):
                kw["scale"] = scale.ap
                rd.append(scale)
            else:
                kw["scale"] = scale
        if bias is not None:
            if isinstance(bias, View):
                kw["bias"] = bias.ap
                rd.append(bias)
            else:
                kw["bias"] = bias
        return self.emit(eng, lambda e: e.activation(out=out.ap, in_=in_.ap, func=func, **kw), [out], rd)

    def tt(self, out, a, b, op, eng="vector"):
        return self.emit(eng, lambda e: e.tensor_tensor(out=out.ap, in0=a.ap, in1=b.ap, op=op), [out], [a, b])

    def ts(self, out, a, s1, op0, s2=None, op1=None, eng="vector"):
        rd = [a]
        v1 = s1.ap if isinstance(s1, View) else s1
        v2 = s2.ap if isinstance(s2, View) else s2
        if isinstance(s1, View):
            rd.append(s1)
        if isinstance(s2, View):
            rd.append(s2)
        if op1 is None:
            return self.emit(eng, lambda e: e.tensor_scalar(out=out.ap, in0=a.ap, scalar1=v1, scalar2=None, op0=op0), [out], rd)
        return self.emit(eng, lambda e: e.tensor_scalar(out=out.ap, in0=a.ap, scalar1=v1, scalar2=v2, op0=op0, op1=op1), [out], rd)

    def stt(self, out, in0, scalar, in1, op0, op1, eng="vector"):
        rd = [in0, in1]
        sv = scalar.ap if isinstance(scalar, View) else scalar
        if isinstance(scalar, View):
            rd.append(scalar)
        return self.emit(eng, lambda e: e.scalar_tensor_tensor(out=out.ap, in0=in0.ap, scalar=sv, in1=in1.ap, op0=op0, op1=op1), [out], rd)

    def scan(self, out, d0, d1, initial, op0=ALU.mult, op1=ALU.add):
        rd = [d0, d1]
        iv = initial.ap if isinstance(initial, View) else initial
        if isinstance(initial, View):
            rd.append(initial)
        return self.emit("vector", lambda e: e.tensor_tensor_scan(out=out.ap, data0=d0.ap, data1=d1.ap, initial=iv, op0=op0, op1=op1), [out], rd)

    def copy(self, out, in_, eng="vector"):
        if eng == "scalar":
            return self.emit(eng, lambda e: e.copy(out=out.ap, in_=in_.ap), [out], [in_])
        return self.emit(eng, lambda e: e.tensor_copy(out=out.ap, in_=in_.ap), [out], [in_])

    def recip(self, out, in_):
        return self.emit("vector", lambda e: e.reciprocal(out=out.ap, in_=in_.ap), [out], [in_])

    def memset(self, out, val, eng="vector"):
        return self.emit(eng, lambda e: e.memset(out.ap, val), [out], [])


def load_weight_block(c, wdst, wsrc_view, q="gpsimd"):
    return c.dma(wdst, wsrc_view, q=q)


def fm_rmsnorm(c, xs, gs, outs, nfeat, ntok, ones, eps_t, ps_ssq, sq_tmp, rstd, after=None):
    nk = len(xs)
    for k in range(nk):
        sq_ = sq_tmp[k % len(sq_tmp)]
        if FP32R_STATS:
            c.act(View(sq_, sq_[:, :ntok].ap.bitcast(F32R)), xs[k], AF.Square)
        else:
            c.act(sq_[:, :ntok], xs[k], AF.Square)
        if FP32R_STATS:
            c.mm(ps_ssq[:, :ntok], View(ones, ones[:, :].ap.bitcast(F32R)), View(sq_, sq_[:, :ntok].ap.bitcast(F32R)),
                 start=(k == 0), stop=(k == nk - 1))
        else:
            c.mm(ps_ssq[:, :ntok], ones[:, :], sq_[:, :ntok], start=(k == 0), stop=(k == nk - 1))
    c.act(rstd[:, :ntok], ps_ssq[:, :ntok], AF.Sqrt, scale=1.0 / nfeat, bias=eps_t[:, 0:1])
    c.recip(rstd[:, :ntok], rstd[:, :ntok])
    for k in range(nk):
        c.stt(outs[k], xs[k], gs[k], rstd[:, :ntok], ALU.mult, ALU.mult)
        if after is not None:
            after(k)


def attn_core(c, qparts, kparts, vaug, nv, scale, mask_fn, psS, psO, E_sb, out_cb, qc_done, itbase=0):
    np_ = len(qparts)
    blocks = [(qc, kb) for qc in range(4) for kb in range(4 * qc + 4)]
    nps, ne = len(psS), len(E_sb)

    def score(i):
        qc, kb = blocks[i]
        ps = psS[(itbase + i) % nps]
        for p in range(np_):
            c.mm(ps[:, :], kparts[p][:, kb * 128:(kb + 1) * 128], qparts[p][:, qc * 512:(qc + 1) * 512],
                 start=(p == 0), stop=(p == np_ - 1))

    score(0)
    for i, (qc, kb) in enumerate(blocks):
        if i + 1 < len(blocks):
            score(i + 1)
        ps = psS[(itbase + i) % nps]
        e = E_sb[(itbase + i) % ne]
        c.act(e[:, :], ps[:, :], AF.Exp, scale=scale)
        m = mask_fn(kb, qc)
        if m is not None:
            c.tt(e[:, :], e[:, :], m, ALU.mult)
        for j in range(4):
            qb = 4 * qc + j
            if kb > qb:
                continue
            c.mm(psO[j][:, :nv + 1], e[:, j * 128:(j + 1) * 128], vaug(kb), start=(kb == 0), stop=(kb == qb))
        if kb == 4 * qc + 3:
            for j in range(4):
                out_cb(qc, j, psO[j])
            qc_done(qc)
    return itbase + len(blocks)


def attn_core_gen(c, qparts, kparts, pv, nv, scale, mask_fn, psS, E_sb, out_cb, qc_done, itbase=0, mask_eng="vector"):
    np_ = len(qparts)
    blocks = [(qc, kb) for qc in range(4) for kb in range(4 * qc + 4)]
    nps, ne = len(psS), len(E_sb)

    def score(i):
        qc, kb = blocks[i]
        ps = psS[(itbase + i) % nps]
        for p in range(np_):
            c.mm(ps[:, :], kparts[p][:, kb * 128:(kb + 1) * 128], qparts[p][:, qc * 512:(qc + 1) * 512],
                 start=(p == 0), stop=(p == np_ - 1))

    score(0)
    for i, (qc, kb) in enumerate(blocks):
        if i + 1 < len(blocks):
            score(i + 1)
        ps = psS[(itbase + i) % nps]
        e = E_sb[(itbase + i) % ne]
        c.act(e[:, :], ps[:, :], AF.Exp, scale=scale)
        m = mask_fn(kb, qc)
        if m is not None:
            me = mask_eng if isinstance(mask_eng, str) else mask_eng[i % len(mask_eng)]
            c.tt(e[:, :], e[:, :], m, ALU.mult, eng=me)
        for j in range(4):
            qb = 4 * qc + j
            if kb > qb:
                continue
            pv(e[:, j * 128:(j + 1) * 128], kb, j, qb)
        if kb == 4 * qc + 3:
            for j in range(4):
                out_cb(qc, j)
            qc_done(qc)
        yield


TOK = 1024
A_TILES = ([(i * 128, 128, False) for i in range(4)] + [(512 + i * 128, 128, False) for i in range(2)]
           + [(768, 64, False), (768, 64, True)]
           + [(832 + i * 128, 128, False) for i in range(12)])
NT_A = len(A_TILES)
R1 = 3072
R2 = 1024


def vdtm_view(buf, row0):
    return View(buf, buf.t.ap()[row0:row0 + 512, :].rearrange("r (two c) -> (r two) c", two=2))


CR1 = 512
CR2 = 256


def s1rows(P, row0, n):
    ch, w = row0 // CR1, row0 % CR1
    assert w + n <= CR1
    b = P["src1"][ch]
    return View(b, b.t.ap()[w:w + n, :])


def g1rows(P, r, row0, n):
    ch, w = row0 // CR1, row0 % CR1
    assert w + n <= CR1
    b = P["G1"][ch]
    return View(b, b.t.ap()[r * CR1 + w:r * CR1 + w + n, :])


def s2rows(P, row0, n):
    ch, w = row0 // CR2, row0 % CR2
    assert w + n <= CR2
    b = P["src2"][ch]
    return View(b, b.t.ap()[w:w + n, :])


def g2rows(P, r, row0, n):
    ch, w = row0 // CR2, row0 % CR2
    assert w + n <= CR2
    b = P["G2"][ch]
    return View(b, b.t.ap()[r * CR2 + w:r * CR2 + w + n, :])


NWB = 5


def phase_A(c, P, l):
    c.begin_phase()
    xsrc = P["xT"] if l == 0 else P["xs"]
    ones, eps_t = P["ones"], P["eps_t"]
    x_sb = c.sb("x_sb", [128, 16, TOK], F32, n=16)
    h_sb = c.sb("h_sb", [128, 16, TOK], BF16, n=16)
    g_sb = c.sb("g_sb", [128, 16], F32)
    sq_tmp = [c.sb(f"sq{i}", [128, 512], BF16) for i in range(3)]
    rstd = c.sb("rstd", [128, 512], F32)
    wblk = [c.sb(f"wblk{i}", [128, 16, 128], BF16) for i in range(NWB)]
    wvd = c.sb("wvd", [128, 16, 512], BF16)
    wrot = c.sb("wrot", [128, 16, 64], BF16)
    osb = [c.sb(f"osb{i}", [128, 512], F32) for i in range(3)]
    ps_ssq = c.ps("ps_ssq", [128, 512])
    pso = [c.ps(f"pso{i}", [128, 512]) for i in range(3)]

    xv = xsrc.ap().rr("(k p) t -> p k t", p=128)
    for k in range(16):
        c.dma(x_sb[k][:, k, :], xv[:, k, :], q="sync")
    c.dma(g_sb[:, :], View(P["gmix"], P["gmix"].t.ap()[l]), q="sync")
    wv = View(P["w_in"], P["w_in"].t.ap()[l].rearrange("(k p) n -> p k n", p=128))

    def load_w(j):
        c0, m, rot = A_TILES[j]
        c.dma(wblk[j % NWB][:, :, :m], wv[:, :, c0:c0 + m], q="gpsimd")

    for j0 in range(NWB - 1):
        load_w(j0)
    for tc in range(2):
        sl = slice(tc * 512, (tc + 1) * 512)
        fm_rmsnorm(c, [x_sb[k][:, k, sl] for k in range(16)], [g_sb[:, k:k + 1] for k in range(16)],
                   [h_sb[k][:, k, sl] for k in range(16)], D, 512, ones, eps_t, ps_ssq, sq_tmp, rstd)
    for q4 in range(4):
        c.dma(wvd[:, 4 * q4:4 * q4 + 4, :], wv[:, 4 * q4:4 * q4 + 4, 2368:2880], q="gpsimd")

    it = 0
    for j in range(NT_A):
        c0, m, rot = A_TILES[j]
        if j + NWB - 1 < NT_A:
            load_w(j + NWB - 1)
        wt = wblk[j % NWB]
        if rot:
            c.act(wrot[:, :, 0:32], wt[:, :, 32:64], AF.Copy, scale=-1.0)
            c.copy(wrot[:, :, 32:64], wt[:, :, 0:32])
            wt = wrot
        for tc in range(2):
            sl = slice(tc * 512, (tc + 1) * 512)
            p = pso[it % 3]
            o = osb[it % 3]
            for k in range(16):
                c.mm(p[:m, :], wt[:, k, :m], h_sb[k][:, k, sl], start=(k == 0), stop=(k == 15))
            c.copy(o[:m, :], p[:m, :], eng=("vector" if it % 2 == 0 else "scalar"))
            c.dma(s1rows(P, j * 128, m)[:, sl], o[:m, :], q="sync")
            it += 1
        if j % 4 == 2 and j >= 6:
            ch = (j - 6) // 4
            c.allgather(P["G1"][ch], P["src1"][ch])
    vdv = vdtm_view(P["src1"][5], 0)
    for tt in range(8):
        p = pso[it % 3]
        o = osb[it % 3]
        for k in range(16):
            c.mm(p[:, :], h_sb[k][:, k, tt * 128:(tt + 1) * 128], wvd[:, k, :], start=(k == 0), stop=(k == 15))
        c.copy(o[:, :], p[:, :], eng=("vector" if it % 2 == 0 else "scalar"))
        c.dma(vdv[tt * 128:(tt + 1) * 128, :], o[:, :], q="sync")
        it += 1
        if tt == 3:
            c.allgather(P["G1"][4], P["src1"][4])
    c.end_phase()


def phase_B(c, P, l):
    c.begin_phase()
    ones, eps_t, ident = P["ones"], P["eps_t"], P["ident"]
    cqf = [c.sb(f"cqf{i}", [128, 4, 512], F32) for i in range(2)]
    ckvf = [c.sb(f"ckvf{i}", [128, 2, 512], F32) for i in range(2)]
    krf = [c.sb(f"krf{i}", [64, 2, 512], F32) for i in range(2)]
    cs_sb = c.sb("cs_sb", [64, 2, S], F32)
    cm_sb = c.sb("cm_sb", [128, 896], BF16)
    cqn = c.sb("cqn", [128, 4, S], BF16, n=4)
    ckvn = c.sb("ckvn", [128, 2, S], BF16, n=2)
    kpe = c.sb("kpe", [64, S], BF16)
    wq = c.sb("wq", [128, 4, 768], BF16)
    wkv = c.sb("wkv", [128, 2, 1024], BF16)
    wrot = c.sb("wrot", [128, 4, 64], BF16)
    g_sb = c.sb("g_sb", [128, 6], F32)
    sq_tmp = [c.sb(f"sq{i}", [128, 512], BF16) for i in range(3)]
    rstd = c.sb("rstd", [128, 512], F32)
    qn = c.sb("qn", [128, S], BF16)
    qpe = c.sb("qpe", [64, S], BF16)
    kn = c.sb("kn", [128, S], BF16)
    vaug = c.sb("vaug", [128, 16, 129], BF16)
    t1 = c.sb("t1", [64, 512], F32)
    t2 = c.sb("t2", [64, 512], F32)
    E_sb = [c.sb(f"E{i}", [128, 512], BF16) for i in range(3)]
    rec = [c.sb(f"rec{i}", [128, 1], F32) for i in range(2)]
    o_sb = [c.sb(f"o_sb{i}", [128, 128], F32) for i in range(2)]
    oT = [c.sb(f"oT{i}", [128, 512], F32) for i in range(2)]
    psS = [c.ps(f"psS{i}", [128, 512]) for i in range(2)]
    psO = [c.ps(f"psO{i}", [128, 512]) for i in range(4)]
    ps_ssq = c.ps("ps_ssq", [128, 512])
    psT = c.ps("psT", [128, 512])

    c.dma(cs_sb[:, :, :], P["cossin"].ap(), q="sync")
    c.dma(cm_sb[:, :], P["cmask"].ap(), q="sync")
    c.dma(g_sb[:, :], View(P["gvec"], P["gvec"].t.ap()[l]), q="sync")
    c.dma(wq[:, :, :], View(P["w_uq"], P["w_uq"].t.ap()[l].rearrange("(k p) n -> p k n", p=128)), q="gpsimd")
    c.dma(wkv[:, :, :], View(P["w_ukv"], P["w_ukv"].t.ap()[l].rearrange("(k p) n -> p k n", p=128)), q="gpsimd")
    c.memset(vaug[:, :, 128:129], 1.0)

    for tc in range(4):
        r, cc = tc // 2, (tc % 2) * 512
        sl = slice(tc * 512, (tc + 1) * 512)
        cq_, ckv_, kr_ = cqf[tc % 2], ckvf[tc % 2], krf[tc % 2]
        c.dma(cq_[:, :, :], g1rows(P, r, 0, 512)[:, cc:cc + 512].rr("(k p) t -> p k t", p=128), q="sync")
        c.dma(ckv_[:, :, :], g1rows(P, r, 512, 256)[:, cc:cc + 512].rr("(k p) t -> p k t", p=128), q="sync")
        c.dma(kr_[:, :, :], g1rows(P, r, 768, 256)[:, cc:cc + 512].rr("(a p) t -> p a t", p=128)[0:64], q="sync")
        fm_rmsnorm(c, [cq_[:, k, :] for k in range(4)], [g_sb[:, k:k + 1] for k in range(4)],
                   [cqn[k][:, k, sl] for k in range(4)], 512, 512, ones, eps_t, ps_ssq, sq_tmp, rstd)
        fm_rmsnorm(c, [ckv_[:, k, :] for k in range(2)], [g_sb[:, 4 + k:5 + k] for k in range(2)],
                   [ckvn[k][:, k, sl] for k in range(2)], 256, 512, ones, eps_t, ps_ssq, sq_tmp, rstd)
        c.tt(t1[:, :], kr_[:, 0, :], cs_sb[:, 0, sl], ALU.mult)
        c.tt(t2[:, :], kr_[:, 1, :], cs_sb[:, 1, sl], ALU.mult)
        c.tt(kpe[:, sl], t1[:, :], t2[:, :], ALU.add)

    it = 0
    scale = 192.0 ** -0.5
    for h in range(4):
        qb0 = h * 192
        c.act(wrot[:, :, 0:32], wq[:, :, qb0 + 160:qb0 + 192], AF.Copy, scale=-1.0)
        c.copy(wrot[:, :, 32:64], wq[:, :, qb0 + 128:qb0 + 160])
        for tc in range(4):
            sl = slice(tc * 512, (tc + 1) * 512)
            p = psS[0]
            for k in range(4):
                c.mm(p[:, :], wq[:, k, qb0:qb0 + 128], cqn[k][:, k, sl], start=(k == 0), stop=(k == 3))
            c.copy(qn[:, sl], p[:, :], eng="scalar")
            p1 = psS[1]
            for k in range(4):
                c.mm(p1[:64, :], wq[:, k, qb0 + 128:qb0 + 192], cqn[k][:, k, sl], start=(k == 0), stop=(k == 3))
            p2 = psO[0]
            for k in range(4):
                c.mm(p2[:64, :], wrot[:, k, :], cqn[k][:, k, sl], start=(k == 0), stop=(k == 3))
            c.tt(t1[:, :], p1[:64, :], cs_sb[:, 0, sl], ALU.mult)
            c.tt(t2[:, :], p2[:64, :], cs_sb[:, 1, sl], ALU.mult)
            c.tt(qpe[:, sl], t1[:, :], t2[:, :], ALU.add)
            p3 = psO[1]
            for k in range(2):
                c.mm(p3[:, :], wkv[:, k, h * 256:h * 256 + 128], ckvn[k][:, k, sl], start=(k == 0), stop=(k == 1))
            c.copy(kn[:, sl], p3[:, :], eng="vector")
        for kq in range(4):
            p = psO[2 + kq % 2]
            for j in range(4):
                kb = kq * 4 + j
                for k in range(2):
                    c.mm(p[:, j * 128:(j + 1) * 128], ckvn[k][:, k, kb * 128:(kb + 1) * 128],
                         wkv[:, k, h * 256 + 128:h * 256 + 256], start=(k == 0), stop=(k == 1))
            c.copy(vaug[:, kq * 4:(kq + 1) * 4, 0:128], p[:, :].rr("p (j d) -> p j d", j=4), eng="scalar")

        def mask_fn(kb, qc):
            c0 = 512 * qc - 128 * kb
            if c0 >= 128:
                return None
            return cm_sb[:, c0 + 384:c0 + 384 + 512]

        def out_cb(qc, j, po):
            r_ = rec[j % 2]
            o = o_sb[j % 2]
            c.recip(r_[:, :], po[:, 128:129])
            c.ts(o[:, :], po[:, 0:128], r_[:, 0:1], ALU.mult)
            c.transpose(psT[:, j * 128:(j + 1) * 128], o[:, :], ident[:, :])

        def qc_done(qc, h=h):
            ot = oT[qc % 2]
            c.copy(ot[:, :], psT[:, :], eng="vector")
            c.dma(s2rows(P, h * 128, 128)[:, qc * 512:(qc + 1) * 512], ot[:, :], q="sync")

        it = attn_core(c, [qn, qpe], [kn, kpe], lambda kb: vaug[:, kb, :], 128, scale, mask_fn, psS, psO, E_sb,
                       out_cb, qc_done, it)
    c.end_phase()


MAGIC = 12582912.0
TWO_PI = 6.283185307179586
CW1 = 6.28125
CW2 = TWO_PI - CW1
PI_LO = 3.1415925
CH = 256


def phase_C(c, P, l):
    c.begin_phase()
    m_sb = P["m_sb"]
    u_f = c.sb("u_f", [128, 2, S], F32)
    u_t = c.sb("u_t", [128, 2, S], F32)
    u_b = c.sb("u_b", [128, 2, S], BF16)
    p_sb = c.sb("p_sb", [128, 3, 8], F32)
    B_sb = c.sb("B_sb", [128, 2, 8, 128], BF16)
    C_sb = c.sb("C_sb", [128, 2, 8, 128], BF16)
    nC_sb = c.sb("nC_sb", [128, 2, 8, 128], BF16)
    d_sb = c.sb("d_sb", [128, 2], F32)
    j_sb = c.sb("j_sb", [128, 512], F32)
    for r in range(2):
        c.dma(u_f[:, :, r * TOK:(r + 1) * TOK], g1rows(P, r, 1024, 256).rr("(k p) t -> p k t", p=128), q="sync")
        c.dma(u_t[:, :, r * TOK:(r + 1) * TOK], g1rows(P, r, 1280, 256).rr("(k p) t -> p k t", p=128), q="sync")
    c.dma(p_sb[:, :, :], View(P["prm"], P["prm"].t.ap()[l]), q="sync")
    c.dma(B_sb[:, :, :, :].rr("p a k m -> p (a k m)"), View(P["Bblk"], P["Bblk"].t.ap()[l].rearrange("p a k m -> p (a k m)")), q="gpsimd")
    c.dma(C_sb[:, :, :, :].rr("p a k m -> p (a k m)"), View(P["Cblk"], P["Cblk"].t.ap()[l].rearrange("p a k m -> p (a k m)")), q="gpsimd")
    c.dma(d_sb[:, :], View(P["dsk"], P["dsk"].t.ap()[l]), q="sync")
    c.dma(j_sb[:, :], P["jrow"].ap(), q="sync")
    for ut in range(2):
        c.ts(u_f[:, ut, :], u_f[:, ut, :], m_sb[:, 0:1], ALU.mult)
        c.stt(u_f[:, ut, :], u_t[:, ut, :], m_sb[:, 1:2], u_f[:, ut, :], ALU.mult, ALU.add)
        c.copy(u_b[:, ut, :], u_f[:, ut, :], eng="scalar")
    c.act(nC_sb[:, :, :, :].rr("p a k m -> p (a k m)"), C_sb[:, :, :, :].rr("p a k m -> p (a k m)"), AF.Copy, scale=-1.0)

    def small(name, n=8):
        return c.sb(name, [128, n], F32)

    kt = c.sb("kt", [128, 512], F32)
    red = c.sb("red", [128, 512], F32)
    ang = c.sb("ang", [128, 512], F32)

    def sin_of(out, a, n, shift=0.0):
        src = a
        if shift != 0.0:
            c.ts(ang[:, :n], a, shift, ALU.add)
            src = ang[:, :n]
        c.ts(kt[:, :n], src, 1.0 / TWO_PI, ALU.mult, MAGIC, ALU.add)
        c.ts(kt[:, :n], kt[:, :n], -MAGIC, ALU.add)
        c.stt(red[:, :n], kt[:, :n], -CW1, src, ALU.mult, ALU.add)
        c.stt(red[:, :n], kt[:, :n], -CW2, red[:, :n], ALU.mult, ALU.add)
        c.ts(red[:, :n], red[:, :n], -PI_LO, ALU.max, PI_LO, ALU.min)
        c.act(out, red[:, :n], AF.Sin)

    lre, dt, ldr, th, rr_ = small("lre"), small("dt"), small("ldr"), small("th"), small("rr")
    cth, sth, ar, ai, arm1 = small("cth"), small("sth"), small("ar"), small("ai"), small("arm1")
    nr, ni, den, tmp, cr, ci = small("nr"), small("ni"), small("den"), small("tmp"), small("cr"), small("ci")
    thT, cT, sT, nsT = small("thT"), small("cT"), small("sT"), small("nsT")
    are, aim, ldt = p_sb[:, 0, :], p_sb[:, 1, :], p_sb[:, 2, :]
    c.ts(lre[:, :], are, -1e-4, ALU.min)
    c.act(dt[:, :], ldt, AF.Exp)
    c.tt(ldr[:, :], lre[:, :], dt[:, :], ALU.mult)
    c.tt(th[:, :], aim, dt[:, :], ALU.mult)
    c.act(rr_[:, :], ldr[:, :], AF.Exp)
    sin_of(sth[:, :], th[:, :], 8)
    sin_of(cth[:, :], th[:, :], 8, shift=np.pi / 2)
    c.tt(ar[:, :], rr_[:, :], cth[:, :], ALU.mult)
    c.tt(ai[:, :], rr_[:, :], sth[:, :], ALU.mult)
    c.ts(arm1[:, :], ar[:, :], -1.0, ALU.add)
    c.tt(nr[:, :], arm1[:, :], lre[:, :], ALU.mult)
    c.tt(tmp[:, :], ai[:, :], aim, ALU.mult)
    c.tt(nr[:, :], nr[:, :], tmp[:, :], ALU.add)
    c.tt(ni[:, :], ai[:, :], lre[:, :], ALU.mult)
    c.tt(tmp[:, :], arm1[:, :], aim, ALU.mult)
    c.tt(ni[:, :], ni[:, :], tmp[:, :], ALU.subtract)
    c.tt(den[:, :], lre[:, :], lre[:, :], ALU.mult)
    c.tt(tmp[:, :], aim, aim, ALU.mult)
    c.tt(den[:, :], den[:, :], tmp[:, :], ALU.add)
    c.recip(den[:, :], den[:, :])
    c.tt(cr[:, :], nr[:, :], den[:, :], ALU.mult)
    c.tt(ci[:, :], ni[:, :], den[:, :], ALU.mult)
    c.ts(thT[:, :], th[:, :], float(CH), ALU.mult)
    sin_of(sT[:, :], thT[:, :], 8)
    sin_of(cT[:, :], thT[:, :], 8, shift=np.pi / 2)
    c.ts(nsT[:, :], sT[:, :], -1.0, ALU.mult)

    Cc = c.sb("Cc", [128, 8, 512], F32, n=8)
    Sn = c.sb("Sn", [128, 8, 512], F32, n=8)
    Tr = c.sb("Tr", [128, 8, 512], F32, n=8)
    Ti = c.sb("Ti", [128, 8, 512], F32, n=8)
    tang = c.sb("tang", [128, 512], F32)
    ttmp = c.sb("ttmp", [128, 512], F32)
    for k in range(8):
        c.ts(tang[:, :], j_sb[:, :], th[:, k:k + 1], ALU.mult)
        sin_of(Sn[k][:, k, :], tang[:, :], 512)
        sin_of(Cc[k][:, k, :], tang[:, :], 512, shift=np.pi / 2)
        c.ts(ttmp[:, :], Cc[k][:, k, :], cr[:, k:k + 1], ALU.mult)
        c.stt(Tr[k][:, k, :], Sn[k][:, k, :], ci[:, k:k + 1], ttmp[:, :], ALU.mult, ALU.add)
        c.ts(ttmp[:, :], Sn[k][:, k, :], cr[:, k:k + 1], ALU.mult)
        c.stt(Ti[k][:, k, :], Cc[k][:, k, :], ci[:, k:k + 1], ttmp[:, :], ALU.mult, ALU.subtract)

    NBUF = 2
    t1 = [c.sb(f"t1_{i}", [128, 512], F32) for i in range(NBUF)]
    t2 = [c.sb(f"t2_{i}", [128, 512], F32) for i in range(NBUF)]
    t3 = [c.sb(f"t3_{i}", [128, 512], F32) for i in range(NBUF)]
    t4 = [c.sb(f"t4_{i}", [128, 512], F32) for i in range(NBUF)]
    xr = [c.sb(f"xr_{i}", [128, 512], F32) for i in range(NBUF)]
    xi = [c.sb(f"xi_{i}", [128, 512], F32) for i in range(NBUF)]
    gr = [c.sb(f"gr_{i}", [128, 512], F32) for i in range(NBUF)]
    gi = [c.sb(f"gi_{i}", [128, 512], F32) for i in range(NBUF)]
    Pp = [[c.sb(f"P{j}_{i}", [128, 512], BF16) for i in range(NBUF)] for j in range(4)]
    init = [c.sb(f"init{k}", [128, 2], F32) for k in range(8)]
    itmp = c.sb("itmp", [128, 2], F32)
    ysb = [c.sb(f"ysb{i}", [128, 512], F32) for i in range(2)]
    gsb = [c.sb(f"gsb{i}", [128, 512], F32) for i in range(2)]
    psx = [c.ps(f"psx{i}", [128, 512]) for i in range(4)]
    psy = [c.ps(f"psy{i}", [128, 512]) for i in range(2)]

    it = 0
    for ut in range(2):
        for tb in range(4):
            sl = slice(tb * 512, (tb + 1) * 512)
            py = psy[(ut * 4 + tb) % 2]
            for kk in range(4):
                k = ut * 4 + kk
                b = it % NBUF
                pr, pi_ = psx[(2 * it) % 4], psx[(2 * it + 1) % 4]
                c.mm(pr[:, :], B_sb[:, 0, k, :], u_b[:, ut, sl])
                c.mm(pi_[:, :], B_sb[:, 1, k, :], u_b[:, ut, sl])
                c.tt(t1[b][:, :], pr[:, :], Tr[k][:, k, :], ALU.mult)
                c.tt(t2[b][:, :], pi_[:, :], Ti[k][:, k, :], ALU.mult)
                c.tt(t3[b][:, :], pi_[:, :], Tr[k][:, k, :], ALU.mult)
                c.tt(t4[b][:, :], pr[:, :], Ti[k][:, k, :], ALU.mult)
                c.tt(xr[b][:, :], t1[b][:, :], t2[b][:, :], ALU.subtract, eng="gpsimd")
                c.tt(xi[b][:, :], t3[b][:, :], t4[b][:, :], ALU.add, eng="gpsimd")
                for sub in range(2):
                    ss = slice(sub * CH, (sub + 1) * CH)
                    first = (tb == 0 and sub == 0)
                    i_r = 0.0 if first else init[k][:, 0:1]
                    i_i = 0.0 if first else init[k][:, 1:2]
                    c.scan(gr[b][:, ss], rr_[:, k:k + 1].bcast([128, CH]), xr[b][:, ss], i_r)
                    c.scan(gi[b][:, ss], rr_[:, k:k + 1].bcast([128, CH]), xi[b][:, ss], i_i)
                    if not (tb == 3 and sub == 1):
                        fr = gr[b][:, (sub + 1) * CH - 1:(sub + 1) * CH]
                        fi = gi[b][:, (sub + 1) * CH - 1:(sub + 1) * CH]
                        c.ts(itmp[:, 0:1], fr, cT[:, k:k + 1], ALU.mult)
                        c.ts(itmp[:, 1:2], fr, sT[:, k:k + 1], ALU.mult)
                        c.stt(init[k][:, 0:1], fi, nsT[:, k:k + 1], itmp[:, 0:1], ALU.mult, ALU.add)
                        c.stt(init[k][:, 1:2], fi, cT[:, k:k + 1], itmp[:, 1:2], ALU.mult, ALU.add)
                c.tt(Pp[0][b][:, :], Cc[k][:, k, :], gr[b][:, :], ALU.mult, eng="gpsimd")
                c.tt(Pp[1][b][:, :], Sn[k][:, k, :], gi[b][:, :], ALU.mult, eng="gpsimd")
                c.tt(Pp[2][b][:, :], Sn[k][:, k, :], gr[b][:, :], ALU.mult)
                c.tt(Pp[3][b][:, :], Cc[k][:, k, :], gi[b][:, :], ALU.mult)
                c.mm(py[:, :], C_sb[:, 0, k, :], Pp[0][b][:, :], start=(kk == 0), stop=False)
                c.mm(py[:, :], nC_sb[:, 0, k, :], Pp[1][b][:, :], start=False, stop=False)
                c.mm(py[:, :], nC_sb[:, 1, k, :], Pp[2][b][:, :], start=False, stop=False)
                c.mm(py[:, :], nC_sb[:, 1, k, :], Pp[3][b][:, :], start=False, stop=(kk == 3))
                it += 1
            yb = ysb[(ut * 4 + tb) % 2]
            gb = gsb[(ut * 4 + tb) % 2]
            c.stt(yb[:, :], u_f[:, ut, sl], d_sb[:, ut:ut + 1], py[:, :], ALU.mult, ALU.add)
            c.act(gb[:, :], yb[:, :], AF.Gelu_apprx_tanh)
            c.dma(s2rows(P, 512 + ut * 128, 128)[:, sl], gb[:, :], q="sync")
    c.end_phase()


def phase_D(c, P, l):
    c.begin_phase()
    m_sb, ident = P["m_sb"], P["ident"]
    qc_ = [c.sb(f"qc{i}", [128, 2, S], F32) for i in range(2)]
    kc_ = [c.sb(f"kc{i}", [128, 2, S], F32) for i in range(2)]
    vc_ = [c.sb(f"vc{i}", [128, 16, 256], F32) for i in range(2)]
    q_bf = c.sb("q_bf", [128, 2, S], BF16)
    k_bf = c.sb("k_bf", [128, 2, S], BF16)
    vaug = c.sb("vaug", [128, 16, 4, 65], BF16)
    dm_sb = c.sb("dm_sb", [128, 2432], BF16)
    E_sb = [c.sb(f"E{i}", [128, 512], BF16) for i in range(3)]
    rec = [c.sb(f"rec{i}", [128, 1], F32) for i in range(2)]
    o_sb = [c.sb(f"o_sb{i}", [128, 64], F32) for i in range(2)]
    oT = [c.sb(f"oT{i}", [64, 512], F32) for i in range(2)]
    psS = [c.ps(f"psS{i}", [128, 512]) for i in range(2)]
    psO = [c.ps(f"psO{i}", [128, 512]) for i in range(4)]
    psT = c.ps("psT", [128, 512])

    c.dma(dm_sb[:, :], P["dmask"].ap(), q="sync")
    for r in range(2):
        for s_ in range(2):
            c.dma(qc_[s_][:, :, r * TOK:(r + 1) * TOK], g1rows(P, r, 1536 + s_ * 256, 256).rr("(k p) t -> p k t", p=128), q="sync")
            c.dma(kc_[s_][:, :, r * TOK:(r + 1) * TOK], g1rows(P, r, 2048 + s_ * 256, 256).rr("(k p) t -> p k t", p=128), q="sync")
            vv = vdtm_view(P["G1"][5], r * CR1)
            c.dma(vc_[s_][:, r * 8:(r + 1) * 8, :],
                  View(P["G1"][5], vv.ap.rearrange("(kb p) n -> p kb n", p=128)[:, :, s_ * 256:(s_ + 1) * 256]), q="sync")
    c.memset(vaug[:, :, :, 64:65], 1.0)
    for t in range(2):
        c.ts(qc_[0][:, t, :], qc_[0][:, t, :], m_sb[:, 0:1], ALU.mult)
        c.stt(q_bf[:, t, :], qc_[1][:, t, :], m_sb[:, 1:2], qc_[0][:, t, :], ALU.mult, ALU.add)
        c.ts(kc_[0][:, t, :], kc_[0][:, t, :], m_sb[:, 0:1], ALU.mult)
        c.stt(k_bf[:, t, :], kc_[1][:, t, :], m_sb[:, 1:2], kc_[0][:, t, :], ALU.mult, ALU.add)
    v0 = vc_[0][:, :, :].rr("p k n -> p (k n)")
    v1 = vc_[1][:, :, :].rr("p k n -> p (k n)")
    c.ts(v0, v0, m_sb[:, 0:1], ALU.mult)
    c.stt(v0, v1, m_sb[:, 1:2], v0, ALU.mult, ALU.add)
    for h in range(4):
        c.copy(vaug[:, :, h, 0:64], vc_[0][:, :, h * 64:(h + 1) * 64], eng="scalar")

    it = 0
    for h in range(4):
        t, r0 = h // 2, (h % 2) * 64

        def mask_fn(kb, qc):
            c0 = 512 * qc - 128 * kb
            return dm_sb[:, c0 + 384:c0 + 384 + 512]

        def out_cb(qc, j, po):
            r_ = rec[j % 2]
            o = o_sb[j % 2]
            c.recip(r_[:, :], po[:, 64:65])
            c.ts(o[:, :], po[:, 0:64], r_[:, 0:1], ALU.mult)
            c.transpose(psT[:64, j * 128:(j + 1) * 128], o[:, :], ident[:, :])

        def qc_done(qc, h=h):
            ot = oT[qc % 2]
            c.copy(ot[:, :], psT[:64, :], eng="vector")
            c.dma(s2rows(P, 768 + h * 64, 64)[:, qc * 512:(qc + 1) * 512], ot[:, :], q="sync")

        it = attn_core(c, [q_bf[r0:r0 + 64, t, :]], [k_bf[r0:r0 + 64, t, :]], lambda kb, h=h: vaug[:, kb, h, :],
                       64, 0.125, mask_fn, psS, psO, E_sb, out_cb, qc_done, it)
    c.end_phase()


def phase_CD(c, P, l):
    c.begin_phase()
    m_sb, ident = P["m_sb"], P["ident"]
    q_bf = c.sb("q_bf", [128, 2, S], BF16)
    k_bf = c.sb("k_bf", [128, 2, S], BF16)
    vaug = c.sb("vaug", [128, 16, 4, 65], BF16)
    dm_sb = c.sb("dm_sb", [128, 2432], BF16)
    E_sb = [c.sb(f"E{i}", [128, 512], BF16) for i in range(3)]
    rec = [c.sb(f"rec{i}", [128, 1], F32) for i in range(2)]
    o_sb = [c.sb(f"o_sb{i}", [128, 64], F32) for i in range(2)]
    oT = [c.sb(f"oT{i}", [64, 512], F32) for i in range(2)]
    psS = [c.ps(f"psS{i}", [128, 512]) for i in range(2)]
    psO = c.ps("psO", [128, 512])
    psT = c.ps("psT", [128, 512])
    psx = [c.ps(f"psx{i}", [128, 512]) for i in range(2)]
    psy = [c.ps(f"psy{i}", [128, 512]) for i in range(2)]
    c.dma(dm_sb[:, :], P["dmask"].ap(), q="sync")
    c.memset(vaug[:, :, :, 64:65], 1.0)
    c.push_scope()
    qk = [c.sb(f"qk{i}", [128, 2, TOK], F32) for i in range(2)]
    vc_ = [c.sb(f"vc{i}", [128, 8, 256], F32) for i in range(2)]
    for r in range(2):
        ts_ = slice(r * TOK, (r + 1) * TOK)
        for (row0, dst) in ((1536, q_bf), (2048, k_bf)):
            for s_ in range(2):
                c.dma(qk[s_][:, :, :], g1rows(P, r, row0 + s_ * 256, 256).rr("(k p) t -> p k t", p=128), q="sync")
            for t in range(2):
                c.ts(qk[0][:, t, :], qk[0][:, t, :], m_sb[:, 0:1], ALU.mult)
                c.stt(dst[:, t, ts_], qk[1][:, t, :], m_sb[:, 1:2], qk[0][:, t, :], ALU.mult, ALU.add)
        vv = vdtm_view(P["G1"][5], r * CR1)
        for s_ in range(2):
            c.dma(vc_[s_][:, :, :], View(P["G1"][5], vv.ap.rearrange("(kb p) n -> p kb n", p=128)[:, :, s_ * 256:(s_ + 1) * 256]), q="sync")
        v0 = vc_[0][:, :, :].rr("p k n -> p (k n)")
        v1 = vc_[1][:, :, :].rr("p k n -> p (k n)")
        c.ts(v0, v0, m_sb[:, 0:1], ALU.mult)
        c.stt(v0, v1, m_sb[:, 1:2], v0, ALU.mult, ALU.add)
        for h in range(4):
            c.copy(vaug[:, r * 8:(r + 1) * 8, h, 0:64], vc_[0][:, :, h * 64:(h + 1) * 64], eng="scalar")
    c.pop_scope()

    u_f = c.sb("u_f", [128, 2, S], F32)
    u_b = c.sb("u_b", [128, 2, S], BF16)
    B_sb = c.sb("B_sb", [128, 2, 8, 128], BF16)
    C_sb = c.sb("C_sb", [128, 2, 8, 128], BF16)
    nC_sb = c.sb("nC_sb", [128, 2, 8, 128], BF16)
    d_sb = c.sb("d_sb", [128, 2], F32)

    def small(name, n=8):
        return c.sb(name, [128, n], F32)

    rr_, cT, sT, nsT = small("rr"), small("cT"), small("sT"), small("nsT")
    Cc = c.sb("Cc", [128, 8, CH], F32, n=8)
    Sn = c.sb("Sn", [128, 8, CH], F32, n=8)
    Tr = c.sb("Tr", [128, 8, CH], F32, n=8)
    Ti = c.sb("Ti", [128, 8, CH], F32, n=8)

    def tab2(bufs, k):
        return View(bufs[k], bufs[k].t[:, k, :].unsqueeze(1).broadcast_to([128, 2, CH]))

    def v3(view):
        return view.rr("p (a j) -> p a j", a=2)

    c.push_scope()
    u_t = c.sb("u_t", [128, 2, S], F32)
    p_sb = c.sb("p_sb", [128, 3, 8], F32)
    j_sb = c.sb("j_sb", [128, 512], F32)
    for r in range(2):
        c.dma(u_f[:, :, r * TOK:(r + 1) * TOK], g1rows(P, r, 1024, 256).rr("(k p) t -> p k t", p=128), q="sync")
        c.dma(u_t[:, :, r * TOK:(r + 1) * TOK], g1rows(P, r, 1280, 256).rr("(k p) t -> p k t", p=128), q="sync")
    c.dma(p_sb[:, :, :], View(P["prm"], P["prm"].t.ap()[l]), q="sync")
    c.dma(B_sb[:, :, :, :].rr("p a k m -> p (a k m)"), View(P["Bblk"], P["Bblk"].t.ap()[l].rearrange("p a k m -> p (a k m)")), q="gpsimd")
    c.dma(C_sb[:, :, :, :].rr("p a k m -> p (a k m)"), View(P["Cblk"], P["Cblk"].t.ap()[l].rearrange("p a k m -> p (a k m)")), q="gpsimd")
    c.dma(d_sb[:, :], View(P["dsk"], P["dsk"].t.ap()[l]), q="sync")
    c.dma(j_sb[:, :], P["jrow"].ap(), q="sync")
    for ut in range(2):
        c.ts(u_f[:, ut, :], u_f[:, ut, :], m_sb[:, 0:1], ALU.mult)
        c.stt(u_f[:, ut, :], u_t[:, ut, :], m_sb[:, 1:2], u_f[:, ut, :], ALU.mult, ALU.add)
        c.copy(u_b[:, ut, :], u_f[:, ut, :], eng="scalar")
    c.act(nC_sb[:, :, :, :].rr("p a k m -> p (a k m)"), C_sb[:, :, :, :].rr("p a k m -> p (a k m)"), AF.Copy, scale=-1.0)
    kt = c.sb("kt", [128, CH], F32)
    red = c.sb("red", [128, CH], F32)
    ang = c.sb("ang", [128, CH], F32)

    def sin_of(out, a, n, shift=0.0):
        src = a
        if shift != 0.0:
            c.ts(ang[:, :n], a, shift, ALU.add)
            src = ang[:, :n]
        c.ts(kt[:, :n], src, 1.0 / TWO_PI, ALU.mult, MAGIC, ALU.add)
        c.ts(kt[:, :n], kt[:, :n], -MAGIC, ALU.add)
        c.stt(red[:, :n], kt[:, :n], -CW1, src, ALU.mult, ALU.add)
        c.stt(red[:, :n], kt[:, :n], -CW2, red[:, :n], ALU.mult, ALU.add)
        c.ts(red[:, :n], red[:, :n], -PI_LO, ALU.max, PI_LO, ALU.min)
        c.act(out, red[:, :n], AF.Sin)

    lre, dt, ldr, th = small("lre"), small("dt"), small("ldr"), small("th")
    cth, sth, ar, ai, arm1 = small("cth"), small("sth"), small("ar"), small("ai"), small("arm1")
    nr, ni, den, tmp, cr, ci = small("nr"), small("ni"), small("den"), small("tmp"), small("cr"), small("ci")
    thT = small("thT")
    are, aim, ldt = p_sb[:, 0, :], p_sb[:, 1, :], p_sb[:, 2, :]
    c.ts(lre[:, :], are, -1e-4, ALU.min)
    c.act(dt[:, :], ldt, AF.Exp)
    c.tt(ldr[:, :], lre[:, :], dt[:, :], ALU.mult)
    c.tt(th[:, :], aim, dt[:, :], ALU.mult)
    c.act(rr_[:, :], ldr[:, :], AF.Exp)
    sin_of(sth[:, :], th[:, :], 8)
    sin_of(cth[:, :], th[:, :], 8, shift=np.pi / 2)
    c.tt(ar[:, :], rr_[:, :], cth[:, :], ALU.mult)
    c.tt(ai[:, :], rr_[:, :], sth[:, :], ALU.mult)
    c.ts(arm1[:, :], ar[:, :], -1.0, ALU.add)
    c.tt(nr[:, :], arm1[:, :], lre[:, :], ALU.mult)
    c.tt(tmp[:, :], ai[:, :], aim, ALU.mult)
    c.tt(nr[:, :], nr[:, :], tmp[:, :], ALU.add)
    c.tt(ni[:, :], ai[:, :], lre[:, :], ALU.mult)
    c.tt(tmp[:, :], arm1[:, :], aim, ALU.mult)
    c.tt(ni[:, :], ni[:, :], tmp[:, :], ALU.subtract)
    c.tt(den[:, :], lre[:, :], lre[:, :], ALU.mult)
    c.tt(tmp[:, :], aim, aim, ALU.mult)
    c.tt(den[:, :], den[:, :], tmp[:, :], ALU.add)
    c.recip(den[:, :], den[:, :])
    c.tt(cr[:, :], nr[:, :], den[:, :], ALU.mult)
    c.tt(ci[:, :], ni[:, :], den[:, :], ALU.mult)
    c.ts(thT[:, :], th[:, :], float(CH), ALU.mult)
    sin_of(sT[:, :], thT[:, :], 8)
    sin_of(cT[:, :], thT[:, :], 8, shift=np.pi / 2)
    c.ts(nsT[:, :], sT[:, :], -1.0, ALU.mult)
    tang = c.sb("tang", [128, CH], F32)
    ttmp = c.sb("ttmp", [128, CH], F32)
    for k in range(8):
        c.ts(tang[:, :], j_sb[:, 0:CH], th[:, k:k + 1], ALU.mult)
        sin_of(Sn[k][:, k, :], tang[:, :], CH)
        sin_of(Cc[k][:, k, :], tang[:, :], CH, shift=np.pi / 2)
        c.ts(ttmp[:, :], Cc[k][:, k, :], cr[:, k:k + 1], ALU.mult)
        c.stt(Tr[k][:, k, :], Sn[k][:, k, :], ci[:, k:k + 1], ttmp[:, :], ALU.mult, ALU.add)
        c.ts(ttmp[:, :], Sn[k][:, k, :], cr[:, k:k + 1], ALU.mult)
        c.stt(Ti[k][:, k, :], Cc[k][:, k, :], ci[:, k:k + 1], ttmp[:, :], ALU.mult, ALU.subtract)
    c.pop_scope()

    NBUF = 3
    t1 = [c.sb(f"t1_{i}", [128, 512], F32) for i in range(NBUF)]
    t2 = [c.sb(f"t2_{i}", [128, 512], F32) for i in range(NBUF)]
    t3 = [c.sb(f"t3_{i}", [128, 512], F32) for i in range(NBUF)]
    t4 = [c.sb(f"t4_{i}", [128, 512], F32) for i in range(NBUF)]
    gr = [c.sb(f"gr_{i}", [128, 512], F32) for i in range(NBUF)]
    gi = [c.sb(f"gi_{i}", [128, 512], F32) for i in range(NBUF)]
    Pp = [[c.sb(f"P{j}_{i}", [128, 512], BF16) for i in range(2)] for j in range(4)]
    init = [c.sb(f"init{k}", [128, 2], F32) for k in range(8)]
    itmp = [c.sb(f"itmp{i}", [128, 2], F32) for i in range(2)]
    ysb = [c.sb(f"ysb{i}", [128, 512], F32) for i in range(2)]
    gsb = [c.sb(f"gsb{i}", [128, 512], F32) for i in range(2)]

    units = [(ut, tb, kk) for ut in range(2) for tb in range(4) for kk in range(4)]

    def stage1(u):
        ut, tb, kk = units[u]
        k = ut * 4 + kk
        b = u % NBUF
        sl = slice(tb * 512, (tb + 1) * 512)
        pr, pi_ = psx[0], psx[1]
        c.mm(pr[:, :], B_sb[:, 0, k, :], u_b[:, ut, sl])
        c.mm(pi_[:, :], B_sb[:, 1, k, :], u_b[:, ut, sl])
        c.tt(v3(t1[b][:, :]), v3(pr[:, :]), tab2(Tr, k), ALU.mult)
        c.tt(v3(t2[b][:, :]), v3(pi_[:, :]), tab2(Ti, k), ALU.mult)
        c.tt(v3(t3[b][:, :]), v3(pi_[:, :]), tab2(Tr, k), ALU.mult)
        c.tt(v3(t4[b][:, :]), v3(pr[:, :]), tab2(Ti, k), ALU.mult)
        c.tt(t1[b][:, :], t1[b][:, :], t2[b][:, :], ALU.subtract)
        c.tt(t3[b][:, :], t3[b][:, :], t4[b][:, :], ALU.add)

    def stage2(u):
        ut, tb, kk = units[u]
        k = ut * 4 + kk
        b = u % NBUF
        for sub in range(2):
            ss = slice(sub * CH, (sub + 1) * CH)
            first = (tb == 0 and sub == 0)
            i_r = 0.0 if first else init[k][:, 0:1]
            i_i = 0.0 if first else init[k][:, 1:2]
            c.scan(gr[b][:, ss], rr_[:, k:k + 1].bcast([128, CH]), t1[b][:, ss], i_r)
            c.scan(gi[b][:, ss], rr_[:, k:k + 1].bcast([128, CH]), t3[b][:, ss], i_i)
            if not (tb == 3 and sub == 1):
                fr = gr[b][:, (sub + 1) * CH - 1:(sub + 1) * CH]
                fi = gi[b][:, (sub + 1) * CH - 1:(sub + 1) * CH]
                tm = itmp[sub]
                c.ts(tm[:, 0:1], fr, cT[:, k:k + 1], ALU.mult)
                c.ts(tm[:, 1:2], fr, sT[:, k:k + 1], ALU.mult)
                c.stt(init[k][:, 0:1], fi, nsT[:, k:k + 1], tm[:, 0:1], ALU.mult, ALU.add)
                c.stt(init[k][:, 1:2], fi, cT[:, k:k + 1], tm[:, 1:2], ALU.mult, ALU.add)

    def stage3(u):
        ut, tb, kk = units[u]
        k = ut * 4 + kk
        b = u % NBUF
        pb = u % 2
        sl = slice(tb * 512, (tb + 1) * 512)
        py = psy[(ut * 4 + tb) % 2]
        c.tt(v3(Pp[0][pb][:, :]), tab2(Cc, k), v3(gr[b][:, :]), ALU.mult)
        c.tt(v3(Pp[1][pb][:, :]), tab2(Sn, k), v3(gi[b][:, :]), ALU.mult)
        c.tt(v3(Pp[2][pb][:, :]), tab2(Sn, k), v3(gr[b][:, :]), ALU.mult)
        c.tt(v3(Pp[3][pb][:, :]), tab2(Cc, k), v3(gi[b][:, :]), ALU.mult)
        c.mm(py[:, :], C_sb[:, 0, k, :], Pp[0][pb][:, :], start=(kk == 0), stop=False)
        c.mm(py[:, :], nC_sb[:, 0, k, :], Pp[1][pb][:, :], start=False, stop=False)
        c.mm(py[:, :], nC_sb[:, 1, k, :], Pp[2][pb][:, :], start=False, stop=False)
        c.mm(py[:, :], nC_sb[:, 1, k, :], Pp[3][pb][:, :], start=False, stop=(kk == 3))
        if kk == 3:
            yb = ysb[(ut * 4 + tb) % 2]
            gb = gsb[(ut * 4 + tb) % 2]
            c.stt(yb[:, :], u_f[:, ut, sl], d_sb[:, ut:ut + 1], py[:, :], ALU.mult, ALU.add)
            c.act(gb[:, :], yb[:, :], AF.Gelu_apprx_tanh)
            c.dma(s2rows(P, 512 + ut * 128, 128)[:, sl], gb[:, :], q="sync")

    def c_main():
        n = len(units)
        for s_ in range(n + 2):
            if s_ < n:
                stage1(s_)
            if 0 <= s_ - 1 < n:
                stage2(s_ - 1)
            if 0 <= s_ - 2 < n:
                stage3(s_ - 2)
            yield

    def d_main():
        it = 0
        for h in range(4):
            t, r0 = h // 2, (h % 2) * 64

            def mask_fn(kb, qc):
                c0 = 512 * qc - 128 * kb
                return dm_sb[:, c0 + 384:c0 + 384 + 512]

            def pv(e_view, kb, j, qb, h=h):
                c.mm(psO[:, j * 65:(j + 1) * 65], e_view, vaug[:, kb, h, :], start=(kb == 0 and j == 0), stop=(kb == qb),
                     skip_group_check=True)

            def out_cb(qc, j):
                r_ = rec[j % 2]
                o = o_sb[j % 2]
                c.recip(r_[:, :], psO[:, j * 65 + 64:j * 65 + 65])
                c.ts(o[:, :], psO[:, j * 65:j * 65 + 64], r_[:, 0:1], ALU.mult)
                c.transpose(psT[:64, j * 128:(j + 1) * 128], o[:, :], ident[:, :])

            def qc_done(qc, h=h):
                ot = oT[qc % 2]
                c.copy(ot[:, :], psT[:64, :], eng="vector")
                c.dma(s2rows(P, 768 + h * 64, 64)[:, qc * 512:(qc + 1) * 512], ot[:, :], q="sync")

            yield from attn_core_gen(c, [q_bf[r0:r0 + 64, t, :]], [k_bf[r0:r0 + 64, t, :]], pv, 64, 0.125, mask_fn,
                                     psS, E_sb, out_cb, qc_done, it, mask_eng="vector")
            it += 40

    cg, dg = c_main(), d_main()
    c_left, d_left = True, True
    while c_left or d_left:
        if c_left:
            try:
                next(cg)
            except StopIteration:
                c_left = False
        for _ in range(5):
            if d_left:
                try:
                    next(dg)
                except StopIteration:
                    d_left = False
    c.end_phase()


NF = DFF // 128
FPG = 4
FG = NF // FPG


def e_src_rows(k):
    if k < 8:
        return (k // 4), (k % 4) * 128
    if k < 12:
        return ((k - 8) // 2), 512 + ((k - 8) % 2) * 128
    return ((k - 12) // 2), 768 + ((k - 12) % 2) * 128


def phase_E(c, P, l):
    final = (l == P["nlayers"] - 1)
    c.begin_phase()
    m_sb = P["m_sb"]
    ones, eps_t = P["ones"], P["eps_t"]
    xsrc = P["xT"] if l == 0 else P["xs"]
    out = P["out"] if final else P["xs"]
    nc = c.nc
    x_sb = c.sb("x_sb", [128, 16, TOK], F32, n=16)
    h_sb = c.sb("h_sb", [128, 16, TOK], BF16, n=16)
    yw_t = c.sbt("yw", [128, 16 * 512], F32)
    y_sb = [c.mkbuf(yw_t, f"yw{i}") for i in range(16)]
    hid_t = c.sbt("hid", [128, 2 * FPG * TOK], BF16)
    hidb = [c.mkbuf(hid_t, "hid0"), c.mkbuf(hid_t, "hid1")]
    gsc = [c.sb(f"gsc{i}", [128, 4, 512], F32) for i in range(2)]
    ytmp = [c.sb(f"ytmp{i}", [128, 512], F32) for i in range(2)]
    v_sb = c.sb("v_sb", [128, 56], F32)
    sq_tmp = [c.sb(f"sq{i}", [128, 512], BF16) for i in range(3)]
    rstd = c.sb("rstd", [128, 512], F32)
    sig = [c.sb(f"sig{i}", [128, 512], F32) for i in range(2)]
    wk = [c.sb(f"wk{i}", [128, 16, 128], BF16) for i in range(4)]
    ps_ssq = c.ps("ps_ssq", [128, 512])
    psA = [c.ps(f"psA{i}", [128, 512]) for i in range(6)]

    def yview(k, n=512):
        return View(y_sb[k], yw_t[:, k * 512:k * 512 + n])

    yw_bf = yw_t[:, :].bitcast(BF16)

    def wdview(slot):
        return View(y_sb[2 * slot], yw_bf[:, slot * 2048:(slot + 1) * 2048])

    hid_ap = hid_t[:, :]

    def hidview(buf, fi, tc):
        off = buf * FPG * TOK + fi * TOK + tc * 512
        return View(hidb[buf], hid_ap[:, off:off + 512])

    def gsview(k):
        return View(hidb[0], hid_ap[:, k * 512:(k + 1) * 512])

    def wgview(k, c0):
        off = FPG * TOK + k * 1024 + c0
        return View(hidb[1], hid_ap[:, off:off + 128])

    xv = xsrc.ap().rr("(k p) t -> p k t", p=128)
    for k in range(16):
        c.dma(x_sb[k][:, k, :], xv[:, k, :], q="sync")
    c.dma(v_sb[:, :], View(P["vecs"], P["vecs"].t.ap()[l]), q="sync")
    c.dma(View(hidb[1], hid_ap[:, FPG * TOK:FPG * TOK + 4096].rearrange("p (k n) -> p k n", k=4)),
          View(P["w_glu"], P["w_glu"].t.ap()[l].rearrange("(k p) n -> p k n", p=128)), q="gpsimd")

    wov = View(P["w_o"], P["w_o"].t.ap()[l].rearrange("(k p) n -> p k n", p=128))
    wgv = View(P["w_gate"], P["w_gate"].t.ap()[l].rearrange("(k p) n -> p k n", p=128))
    wuv = View(P["w_up"], P["w_up"].t.ap()[l].rearrange("(k p) n -> p k n", p=128))
    wdv = View(P["w_down"], P["w_down"].t.ap()[l])

    kblocks = [(wov, m) for m in range(16)]
    for f in range(NF):
        kblocks.append((wgv, f))
        kblocks.append((wuv, f))
    kstate = {"next": 0}

    def prefetch_k(upto):
        while kstate["next"] <= upto and kstate["next"] < len(kblocks):
            i = kstate["next"]
            src, m = kblocks[i]
            c.dma(wk[i % 4][:, :, :], src[:, :, m * 128:(m + 1) * 128], q="gpsimd")
            kstate["next"] += 1

    prefetch_k(2)
    it = 0
    for tc in range(2):
        sl = slice(tc * 512, (tc + 1) * 512)
        def load_blend(k):
            pr_, r0 = e_src_rows(k)
            yt = ytmp[k % 2]
            gsrc = g2rows(P, pr_, r0, 128)
            c.dma(yview(k), gsrc[:, tc * 512:tc * 512 + 512], q="sync")
            c.dma(yt[:, :], gsrc[:, TOK + tc * 512:TOK + tc * 512 + 512], q="sync")
            c.ts(yview(k), yview(k), m_sb[:, 0:1], ALU.mult)
            c.stt(yview(k), yt[:, :], m_sb[:, 1:2], yview(k), ALU.mult, ALU.add)

        def norm_group(k0, nk, nf):
            ks = list(range(k0, k0 + nk))
            fm_rmsnorm(c, [yview(k) for k in ks], [v_sb[:, 8 + k:9 + k] for k in ks],
                       [h_sb[k][:, k, sl] for k in ks], nf, 512, ones, eps_t, ps_ssq, sq_tmp, rstd)

        for k in range(8):
            load_blend(k)
        norm_group(0, 8, 1024)
        for k in range(4):
            pr_, r0 = e_src_rows(8 + k)
            gsrc = g2rows(P, pr_, r0, 128)
            c.dma(gsc[0][:, k, :], gsrc[:, tc * 512:tc * 512 + 512], q="sync")
            c.dma(gsc[1][:, k, :], gsrc[:, TOK + tc * 512:TOK + tc * 512 + 512], q="sync")
        for k in range(4):
            c.ts(gsc[0][:, k, :], gsc[0][:, k, :], m_sb[:, 0:1], ALU.mult)
            c.stt(gsview(k), gsc[1][:, k, :], m_sb[:, 1:2], gsc[0][:, k, :], ALU.mult, ALU.add)
        for m in range(4):
            p1 = psA[(2 * it) % 6]
            p2 = psA[(2 * it + 1) % 6]
            for k in range(4):
                c.mm(p1[:, :], wgview(k, m * 128), gsview(k), start=(k == 0), stop=(k == 3))
            for k in range(4):
                c.mm(p2[:, :], wgview(k, 512 + m * 128), gsview(k), start=(k == 0), stop=(k == 3))
            sg = sig[it % 2]
            c.act(sg[:, :], p2[:, :], AF.Sigmoid, bias=v_sb[:, 4 + m:5 + m])
            c.stt(yview(8 + m), p1[:, :], v_sb[:, m:m + 1], sg[:, :], ALU.add, ALU.mult)
            it += 1
        norm_group(8, 4, 512)
        for k in range(12, 16):
            load_blend(k)
        norm_group(12, 4, 512)

    bi = 0
    for m in range(16):
        prefetch_k(bi + 2)
        wt = wk[bi % 4]
        for tc in range(2):
            sl = slice(tc * 512, (tc + 1) * 512)
            p = psA[it % 6]
            for k in range(16):
                c.mm(p[:, :], wt[:, k, :], h_sb[k][:, k, sl], start=(k == 0), stop=(k == 15))
            c.tt(x_sb[m][:, m, sl], x_sb[m][:, m, sl], p[:, :], ALU.add)
            it += 1
        bi += 1

    for tc in range(2):
        sl = slice(tc * 512, (tc + 1) * 512)
        fm_rmsnorm(c, [x_sb[k][:, k, sl] for k in range(16)], [v_sb[:, 24 + k:25 + k] for k in range(16)],
                   [h_sb[k][:, k, sl] for k in range(16)], D, 512, ones, eps_t, ps_ssq, sq_tmp, rstd)

    def load_wd(g):
        for fi in range(FPG):
            f = g * FPG + fi
            c.dma(wdview((g % 2) * FPG + fi), wdv[f * 128:(f + 1) * 128, :], q="gpsimd")

    load_wd(0)
    for g in range(FG):
        hb = g % 2
        if g + 1 < FG:
            load_wd(g + 1)
        for fi in range(FPG):
            prefetch_k(bi + 3)
            wgt = wk[bi % 4]
            wut = wk[(bi + 1) % 4]
            for tc in range(2):
                sl = slice(tc * 512, (tc + 1) * 512)
                pg = psA[(2 * it) % 6]
                pu = psA[(2 * it + 1) % 6]
                for k in range(16):
                    c.mm(pg[:, :], wgt[:, k, :], h_sb[k][:, k, sl], start=(k == 0), stop=(k == 15))
                for k in range(16):
                    c.mm(pu[:, :], wut[:, k, :], h_sb[k][:, k, sl], start=(k == 0), stop=(k == 15))
                sg = sig[it % 2]
                c.act(sg[:, :], pg[:, :], AF.Silu)
                c.tt(hidview(hb, fi, tc), sg[:, :], pu[:, :], ALU.mult)
                it += 1
            bi += 2
        for tc in range(2):
            sl = slice(tc * 512, (tc + 1) * 512)
            for mq in range(4):
                pss = [psA[(it + j) % 6] for j in range(4)]
                for fi in range(FPG):
                    wdt = wdview((g % 2) * FPG + fi)
                    for j in range(4):
                        m = mq * 4 + j
                        c.mm(pss[j][:, :], wdt[:, m * 128:(m + 1) * 128], hidview(hb, fi, tc),
                             start=(fi == 0), stop=(fi == FPG - 1))
                for j in range(4):
                    m = mq * 4 + j
                    c.tt(x_sb[m][:, m, sl], x_sb[m][:, m, sl], pss[j][:, :], ALU.add)
                it += 4

    ov = out.ap().rr("(k p) t -> p k t", p=128)
    if final:
        for tc in range(2):
            sl = slice(tc * 512, (tc + 1) * 512)
            osb = [View(hidb[j % 2], hid_t[:, :].bitcast(F32)[:, j * 512:(j + 1) * 512]) for j in range(4)]

            def after(k, sl=sl, osb=osb):
                c.dma(ov[:, k, sl], osb[k % 4], q="sync")
            fm_rmsnorm(c, [x_sb[k][:, k, sl] for k in range(16)], [v_sb[:, 40 + k:41 + k] for k in range(16)],
                       [osb[k % 4] for k in range(16)], D, 512, ones, eps_t, ps_ssq, sq_tmp, rstd, after=after)
    else:
        for k in range(16):
            c.dma(ov[:, k, :], x_sb[k][:, k, :], q="sync")
    c.end_phase()


def build_fused(nlayers=DEPTH):
    c = Ctx()
    P = {"nlayers": nlayers}
    L = nlayers
    P["xT"] = c.dram("xT", [D, TOK], F32, "ExternalInput")
    P["w_in"] = c.dram("w_in", [L, D, IN_W], F32, "ExternalInput")
    P["gmix"] = c.dram("gmix", [L, 128, 16], F32, "ExternalInput")
    P["w_uq"] = c.dram("w_uq", [L, 512, 768], F32, "ExternalInput")
    P["w_ukv"] = c.dram("w_ukv", [L, 256, 1024], F32, "ExternalInput")
    P["gvec"] = c.dram("gvec", [L, 128, 6], F32, "ExternalInput")
    P["prm"] = c.dram("prm", [L, 128, 3, 8], F32, "ExternalInput")
    P["Bblk"] = c.dram("Bblk", [L, 128, 2, 8, 128], F32, "ExternalInput")
    P["Cblk"] = c.dram("Cblk", [L, 128, 2, 8, 128], F32, "ExternalInput")
    P["dsk"] = c.dram("dsk", [L, 128, 2], F32, "ExternalInput")
    P["w_glu"] = c.dram("w_glu", [L, 512, 1024], F32, "ExternalInput")
    P["w_o"] = c.dram("w_o", [L, D, D], F32, "ExternalInput")
    P["w_gate"] = c.dram("w_gate", [L, D, DFF], F32, "ExternalInput")
    P["w_up"] = c.dram("w_up", [L, D, DFF], F32, "ExternalInput")
    P["w_down"] = c.dram("w_down", [L, DFF, D], F32, "ExternalInput")
    P["vecs"] = c.dram("vecs", [L, 128, 56], F32, "ExternalInput")
    P["cossin"] = c.dram("cossin", [64, 2, S], F32, "ExternalInput")
    P["cmask"] = c.dram("cmask", [128, 896], BF16, "ExternalInput")
    P["dmask"] = c.dram("dmask", [128, 2432], BF16, "ExternalInput")
    P["jrow"] = c.dram("jrow", [128, 512], F32, "ExternalInput")
    identd = c.dram("ident", [128, 128], F32, "ExternalInput")
    mseld = c.dram("msel", [128, 2], F32, "ExternalInput")
    P["out"] = c.dram("outT", [D, TOK], F32, "ExternalOutput")
    P["src1"] = [c.dram(f"src1_{i}", [CR1, TOK], F32) for i in range(R1 // CR1)]
    P["G1"] = [c.dram(f"G1_{i}", [2 * CR1, TOK], F32) for i in range(R1 // CR1)]
    P["src2"] = [c.dram(f"src2_{i}", [CR2, S], F32) for i in range(R2 // CR2)]
    P["G2"] = [c.dram(f"G2_{i}", [2 * CR2, S], F32) for i in range(R2 // CR2)]
    P["xs"] = c.dram("xs", [D, TOK], F32)

    P["ones"] = c.sb("ones", [128, 128], BF16)
    P["eps_t"] = c.sb("eps_t", [128, 1], F32)
    P["ident"] = c.sb("ident_sb", [128, 128], F32)
    P["m_sb"] = c.sb("m_sb", [128, 2], F32)
    if FP32R_STATS:
        c.memset(View(P["ones"], P["ones"][:, :].ap.bitcast(F32R)), 1.0)
    else:
        c.memset(P["ones"][:, :], 1.0)
    c.memset(P["eps_t"][:, :], EPS)
    c.dma(P["ident"][:, :], identd.ap(), q="sync")
    c.dma(P["m_sb"][:, :], mseld.ap(), q="sync")

    for l in range(nlayers):
        phase_A(c, P, l)
        c.allgather(P["G1"][5], P["src1"][5])
        phase_B(c, P, l)
        c.allgather(P["G2"][0], P["src2"][0])
        c.allgather(P["G2"][1], P["src2"][1])
        phase_CD(c, P, l)
        c.allgather(P["G2"][2], P["src2"][2])
        c.allgather(P["G2"][3], P["src2"][3])
        phase_E(c, P, l)
    c.barrier()
    c.finish([P["out"]])
    return c.nc


def rope_tables():
    half = 32
    inv = (10000.0 ** (-np.arange(half, dtype=np.float32) / half)).astype(np.float32)
    ang = (np.arange(S, dtype=np.float32)[:, None] * inv[None, :]).astype(np.float32)
    cos = np.cos(ang.astype(np.float64)).astype(np.float32).T
    sin = np.sin(ang.astype(np.float64)).astype(np.float32).T
    cs = np.stack([np.concatenate([cos, cos], 0), np.concatenate([sin, sin], 0)], axis=1)
    return np.ascontiguousarray(cs)


def mask_tables():
    import ml_dtypes
    p = np.arange(128)[:, None]
    cc = np.arange(896)[None, :] - 384
    causal = ((cc - p) >= 0).astype(np.float32).astype(ml_dtypes.bfloat16)
    cc = np.arange(2432)[None, :] - 384
    dl = cc - p
    m = ((dl >= 0) & (dl <= 128)).astype(np.float32)
    m += ((dl >= 0) & (dl <= 512) & (dl % 4 == 0)).astype(np.float32)
    m += ((dl >= 0) & (dl % 16 == 0)).astype(np.float32)
    return causal, m.astype(ml_dtypes.bfloat16)


def ssm_layout(a_re, a_im, log_dt, b_re, b_im, c_re, c_im, d_skip, gh):
    g0 = gh * 16
    prm = np.zeros((128, 3, 8), np.float32)
    Bb = np.zeros((128, 2, 8, 128), np.float32)
    Cb = np.zeros((128, 2, 8, 128), np.float32)
    dsk = np.zeros((128, 2), np.float32)
    for k in range(8):
        for g2 in range(2):
            g = g0 + 2 * k + g2
            ps = slice(g2 * 64, (g2 + 1) * 64)
            prm[ps, 0, k] = a_re[g]
            prm[ps, 1, k] = a_im[g]
            prm[ps, 2, k] = log_dt[g]
            r0 = (k % 4) * 32 + g2 * 16
            Bb[r0:r0 + 16, 0, k, ps] = b_re[g].T
            Bb[r0:r0 + 16, 1, k, ps] = b_im[g].T
            Cb[ps, 0, k, r0:r0 + 16] = c_re[g].T
            Cb[ps, 1, k, r0:r0 + 16] = c_im[g].T
            dsk[r0:r0 + 16, k // 4] = d_skip[g]
    return prm, Bb, Cb, dsk


_CACHE = {}


def get_nc(name, fn):
    if name not in _CACHE:
        _CACHE[name] = fn()
    return _CACHE[name]


def run(nc, in_maps):
    res = run_bass_kernel_spmd(nc, in_maps, core_ids=list(range(NCORES)))
    return res.results


def _col(v):
    return np.ascontiguousarray(np.asarray(v, np.float32).reshape(-1, 128).T)


def make_inputs(inp, L=DEPTH):
    x = inp["x"]
    causal, dmask = mask_tables()
    cs = rope_tables()
    jrow = np.tile(np.tile(np.arange(256, dtype=np.float32), 2)[None, :], (128, 1))
    ident = np.eye(128, dtype=np.float32)
    gmix = np.stack([_col(inp["g_mix"][l]) for l in range(L)])
    gvec = np.stack([np.concatenate([_col(inp["g_q"][l]), _col(inp["g_kv"][l])], axis=1) for l in range(L)])
    vecs = np.stack([np.concatenate([_col(inp["b_glu"][l]),
                                     _col(np.concatenate([inp["g_out_mla"][l], inp["g_out_ssm"][l], inp["g_out_dil"][l]])),
                                     _col(inp["g_ffn"][l]), _col(inp["g_final"])], axis=1) for l in range(L)])
    shared = {"w_in": np.ascontiguousarray(inp["w_in"][:L]), "gmix": gmix, "gvec": gvec, "vecs": vecs,
              "w_glu": np.ascontiguousarray(inp["w_glu"][:L]), "w_o": np.ascontiguousarray(inp["w_o"][:L]),
              "w_gate": np.ascontiguousarray(inp["w_gate"][:L]), "w_up": np.ascontiguousarray(inp["w_up"][:L]),
              "w_down": np.ascontiguousarray(inp["w_down"][:L]), "cossin": cs, "cmask": causal, "dmask": dmask,
              "jrow": jrow, "ident": ident}
    half = []
    for hg in range(2):
        ss = [ssm_layout(inp["a_re"][l], inp["a_im"][l], inp["log_dt"][l], inp["b_re"][l], inp["b_im"][l],
                         inp["c_re"][l], inp["c_im"][l], inp["d_skip"][l], hg) for l in range(L)]
        msel = np.zeros((128, 2), np.float32)
        msel[:, hg] = 1.0
        half.append({"w_uq": np.ascontiguousarray(inp["w_uq"][:L, :, hg * 768:(hg + 1) * 768]),
                     "w_ukv": np.ascontiguousarray(inp["w_ukv"][:L, :, hg * 1024:(hg + 1) * 1024]),
                     "prm": np.stack([s_[0] for s_ in ss]), "Bblk": np.stack([s_[1] for s_ in ss]),
                     "Cblk": np.stack([s_[2] for s_ in ss]), "dsk": np.stack([s_[3] for s_ in ss]), "msel": msel})
    ims = []
    for b in range(NB):
        for hf in range(2):
            d = dict(shared)
            d.update(half[hf])
            d["xT"] = np.ascontiguousarray(x[b, hf * TOK:(hf + 1) * TOK].T)
            ims.append(d)
    return ims


def kernel(**inputs):
    inp = {k: np.asarray(v) for k, v in inputs.items()}
    nc = get_nc("fused", build_fused)
    res = run(nc, make_inputs(inp))
    out = np.empty((NB, S, D), np.float32)
    i = 0
    for b in range(NB):
        for hf in range(2):
            out[b, hf * TOK:(hf + 1) * TOK] = res[i]["outT"].T
            i += 1
    return out
```

```python
import numpy as np
import concourse.bass as bass
import concourse.mybir as mybir
from concourse.bass_utils import run_bass_kernel_spmd

F32 = mybir.dt.float32
BF16 = mybir.dt.bfloat16
F32R = mybir.dt.float32r
FP32R_STATS = False
AF = mybir.ActivationFunctionType
ALU = mybir.AluOpType
AX = mybir.AxisListType

D = 2048
S = 2048
NB = 4
DEPTH = 4
IN_W = 2880
DFF = 5632
EPS = 1e-6
NCORES = 8


class DSem:
    def __init__(self, sem):
        self.sem = sem
        self.cnt = 0


class Buf:
    def __init__(self, t, name):
        self.t = t
        self.name = name
        self.w = None
        self.r = {}
        self.dsem = None

    def __getitem__(self, idx):
        return View(self, self.t[idx])

    def ap(self):
        return View(self, self.t.ap() if hasattr(self.t, "ap") else self.t[:])


class View:
    def __init__(self, buf, ap):
        self.buf = buf
        self.ap = ap

    def __getitem__(self, idx):
        return View(self.buf, self.ap[idx])

    def bcast(self, shape):
        return View(self.buf, self.ap.to_broadcast(shape))

    def rr(self, pat, **kw):
        return View(self.buf, self.ap.rearrange(pat, **kw))


class Eng:
    def __init__(self, name, obj, sem):
        self.name = name
        self.obj = obj
        self.sem = sem
        self.cnt = 0
        self.waited = {}


class Ctx:
    def __init__(self):
        self.nc = bass.Bass("TRN2", target_bir_lowering=False)
        nc = self.nc
        self.engs = {}
        for n in ["tensor", "vector", "scalar", "gpsimd", "sync"]:
            self.engs[n] = Eng(n, getattr(nc, n), nc.alloc_semaphore("sem_" + n))
        self.uid = 0
        self.stack = None
        self.free_dsems = []
        self.all_dsems = []
        self.phase_bufs = []
        self.csem = None
        self.ccnt = 0

    def _nm(self, name):
        self.uid += 1
        return f"{name}_{self.uid}"

    def _reg(self, b):
        if self.stack is not None:
            self.phase_bufs.append(b)
        return b

    def sb(self, name, shape, dtype, n=1):
        nm = self._nm(name)
        if self.stack is not None:
            t = self.stack.enter_context(self.nc.sbuf_tensor(nm, list(shape), dtype))
        else:
            t = self.nc.alloc_sbuf_tensor(nm, list(shape), dtype)
        if n == 1:
            return self._reg(Buf(t, nm))
        return [self._reg(Buf(t, f"{nm}_{i}")) for i in range(n)]

    def sbt(self, name, shape, dtype):
        nm = self._nm(name)
        if self.stack is not None:
            return self.stack.enter_context(self.nc.sbuf_tensor(nm, list(shape), dtype))
        return self.nc.alloc_sbuf_tensor(nm, list(shape), dtype)

    def mkbuf(self, t, name):
        return self._reg(Buf(t, self._nm(name)))

    def ps(self, name, shape, dtype=F32):
        nm = self._nm(name)
        if self.stack is not None:
            t = self.stack.enter_context(self.nc.psum_tensor(nm, list(shape), dtype))
        else:
            t = self.nc.alloc_psum_tensor(nm, list(shape), dtype)
        return self._reg(Buf(t, nm))

    def dram(self, name, shape, dtype, kind=None):
        if kind is None:
            t = self.nc.dram_tensor(name, list(shape), dtype)
        else:
            t = self.nc.dram_tensor(name, list(shape), dtype, kind=kind)
        return Buf(t, name)

    def begin_phase(self):
        from contextlib import ExitStack
        assert self.stack is None
        self.stack = ExitStack()
        self.phase_bufs = []
        self.scopes = []

    def push_scope(self):
        from contextlib import ExitStack
        self.scopes.append((self.stack, self.phase_bufs))
        self.stack = ExitStack()
        self.phase_bufs = []

    def pop_scope(self):
        self.barrier()
        for b in self.phase_bufs:
            if b.dsem is not None:
                self.free_dsems.append(b.dsem)
                b.dsem = None
        self.stack.close()
        self.stack, self.phase_bufs = self.scopes.pop()

    def end_phase(self):
        self.barrier()
        for b in self.phase_bufs:
            if b.dsem is not None:
                self.free_dsems.append(b.dsem)
                b.dsem = None
        self.phase_bufs = []
        self.stack.close()
        self.stack = None

    def _get_dsem(self, b):
        if b.dsem is None:
            if self.free_dsems:
                b.dsem = self.free_dsems.pop()
            else:
                b.dsem = DSem(self.nc.alloc_semaphore(self._nm("dsem")))
                self.all_dsems.append(b.dsem)
        return b.dsem

    def barrier(self):
        for eng in self.engs.values():
            for other in self.engs.values():
                if other is not eng and other.cnt > 0:
                    self._wait(eng, ("e", other, other.cnt))
            for ds in self.all_dsems:
                if ds.cnt > 0:
                    self._wait(eng, ("d", ds))
            if self.ccnt > 0:
                self._wait(eng, ("c", self.ccnt))

    def _wait(self, eng, tok):
        kind = tok[0]
        if kind == "e":
            _, src, val = tok
            if src is eng and eng.name == "tensor":
                return
            sem = src.sem
        elif kind == "d":
            ds = tok[1]
            sem = ds.sem
            val = ds.cnt * 16
        else:
            sem = self.csem
            val = tok[1]
        key = id(sem)
        if eng.waited.get(key, 0) >= val:
            return
        eng.waited[key] = val
        eng.obj.wait_ge(sem, val)

    def _deps(self, eng, w, r):
        toks = []
        for b in r:
            if b.w is not None:
                toks.append(b.w)
        for b in w:
            if b.w is not None:
                toks.append(b.w)
            toks.extend(b.r.values())
        for tok in toks:
            self._wait(eng, tok)

    def emit(self, engname, fn, w, r):
        eng = self.engs[engname]
        w = [v.buf if isinstance(v, View) else v for v in w]
        r = [v.buf if isinstance(v, View) else v for v in r if isinstance(v, (View, Buf))]
        self._deps(eng, w, r)
        inst = fn(eng.obj)
        eng.cnt += 1
        inst.then_inc(eng.sem, 1)
        tok = ("e", eng, eng.cnt)
        for b in w:
            b.w = tok
            b.r = {}
        for b in r:
            if b.w is not tok:
                b.r[eng.name] = tok
        return tok

    def dma(self, out, in_, q="sync", **kw):
        eng = self.engs[q]
        ob, ib = out.buf, in_.buf
        self._deps(eng, [ob], [ib])
        ds = self._get_dsem(ob)
        eng.obj.dma_start(out=out.ap, in_=in_.ap, **kw).then_inc(ds.sem, 16)
        ds.cnt += 1
        tok = ("d", ds)
        ob.w = tok
        ob.r = {}
        ib.r[id(ds)] = tok
        return tok

    def allgather(self, dst, src):
        eng = self.engs["gpsimd"]
        self._deps(eng, [dst], [src])
        if self.csem is None:
            self.csem = self.nc.alloc_semaphore("csem")
        eng.obj.collective_compute("AllGather", ALU.bypass, replica_groups=[[0, 1], [2, 3], [4, 5], [6, 7]],
                                   ins=[src.t.ap().opt()], outs=[dst.t.ap().opt()]).then_inc(self.csem)
        self.ccnt += 1
        tok = ("c", self.ccnt)
        dst.w = tok
        dst.r = {}
        src.r["cc"] = tok
        return tok

    def finish(self, out_bufs):
        eng = self.engs["sync"]
        for b in out_bufs:
            if b.w is not None:
                self._wait(eng, b.w)

    def mm(self, out, lhsT, rhs, start=True, stop=True, **kw):
        return self.emit("tensor", lambda e: e.matmul(out.ap, lhsT=lhsT.ap, rhs=rhs.ap, start=start, stop=stop, **kw),
                         [out], [lhsT, rhs])

    def transpose(self, out, in_, ident):
        return self.emit("tensor", lambda e: e.transpose(out.ap, in_.ap, ident.ap), [out], [in_, ident])

    def act(self, out, in_, func, scale=None, bias=None, eng="scalar"):
        kw = {}
        rd = [in_]
        if scale is not None:
            if isinstance(scale, View):
                kw["scale"] = scale.ap
                rd.append(scale)
            else:
                kw["scale"] = scale
        if bias is not None:
            if isinstance(bias, View):
                kw["bias"] = bias.ap
                rd.append(bias)
            else:
                kw["bias"] = bias
        return self.emit(eng, lambda e: e.activation(out=out.ap, in_=in_.ap, func=func, **kw), [out], rd)

    def tt(self, out, a, b, op, eng="vector"):
        return self.emit(eng, lambda e: e.tensor_tensor(out=out.ap, in0=a.ap, in1=b.ap, op=op), [out], [a, b])

    def ts(self, out, a, s1, op0, s2=None, op1=None, eng="vector"):
        rd = [a]
        v1 = s1.ap if isinstance(s1, View) else s1
        v2 = s2.ap if isinstance(s2, View) else s2
        if isinstance(s1, View):
            rd.append(s1)
        if isinstance(s2, View):
            rd.append(s2)
        if op1 is None:
            return self.emit(eng, lambda e: e.tensor_scalar(out=out.ap, in0=a.ap, scalar1=v1, scalar2=None, op0=op0), [out], rd)
        return self.emit(eng, lambda e: e.tensor_scalar(out=out.ap, in0=a.ap, scalar1=v1, scalar2=v2, op0=op0, op1=op1), [out], rd)

    def stt(self, out, in0, scalar, in1, op0, op1, eng="vector"):
        rd = [in0, in1]
        sv = scalar.ap if isinstance(scalar, View) else scalar
        if isinstance(scalar, View):
            rd.append(scalar)
        return self.emit(eng, lambda e: e.scalar_tensor_tensor(out=out.ap, in0=in0.ap, scalar=sv, in1=in1.ap, op0=op0, op1=op1), [out], rd)

    def scan(self, out, d0, d1, initial, op0=ALU.mult, op1=ALU.add):
        rd = [d0, d1]
        iv = initial.ap if isinstance(initial, View) else initial
        if isinstance(initial, View):
            rd.append(initial)
        return self.emit("vector", lambda e: e.tensor_tensor_scan(out=out.ap, data0=d0.ap, data1=d1.ap, initial=iv, op0=op0, op1=op1), [out], rd)

    def copy(self, out, in_, eng="vector"):
        if eng == "scalar":
            return self.emit(eng, lambda e: e.copy(out=out.ap, in_=in_.ap), [out], [in_])
        return self.emit(eng, lambda e: e.tensor_copy(out=out.ap, in_=in_.ap), [out], [in_])

    def recip(self, out, in_):
        return self.emit("vector", lambda e: e.reciprocal(out=out.ap, in_=in_.ap), [out], [in_])

    def memset(self, out, val, eng="vector"):
        return self.emit(eng, lambda e: e.memset(out.ap, val), [out], [])


def load_weight_block(c, wdst, wsrc_view, q="gpsimd"):
    return c.dma(wdst, wsrc_view, q=q)


def fm_rmsnorm(c, xs, gs, outs, nfeat, ntok, ones, eps_t, ps_ssq, sq_tmp, rstd, after=None):
    nk = len(xs)
    for k in range(nk):
        sq_ = sq_tmp[k % len(sq_tmp)]
        if FP32R_STATS:
            c.act(View(sq_, sq_[:, :ntok].ap.bitcast(F32R)), xs[k], AF.Square)
        else:
            c.act(sq_[:, :ntok], xs[k], AF.Square)
        if FP32R_STATS:
            c.mm(ps_ssq[:, :ntok], View(ones, ones[:, :].ap.bitcast(F32R)), View(sq_, sq_[:, :ntok].ap.bitcast(F32R)),
                 start=(k == 0), stop=(k == nk - 1))
        else:
            c.mm(ps_ssq[:, :ntok], ones[:, :], sq_[:, :ntok], start=(k == 0), stop=(k == nk - 1))
    c.act(rstd[:, :ntok], ps_ssq[:, :ntok], AF.Sqrt, scale=1.0 / nfeat, bias=eps_t[:, 0:1])
    c.recip(rstd[:, :ntok], rstd[:, :ntok])
    for k in range(nk):
        c.stt(outs[k], xs[k], gs[k], rstd[:, :ntok], ALU.mult, ALU.mult)
        if after is not None:
            after(k)


def attn_core(c, qparts, kparts, vaug, nv, scale, mask_fn, psS, psO, E_sb, out_cb, qc_done, itbase=0):
    np_ = len(qparts)
    blocks = [(qc, kb) for qc in range(4) for kb in range(4 * qc + 4)]
    nps, ne = len(psS), len(E_sb)

    def score(i):
        qc, kb = blocks[i]
        ps = psS[(itbase + i) % nps]
        for p in range(np_):
            c.mm(ps[:, :], kparts[p][:, kb * 128:(kb + 1) * 128], qparts[p][:, qc * 512:(qc + 1) * 512],
                 start=(p == 0), stop=(p == np_ - 1))

    score(0)
    for i, (qc, kb) in enumerate(blocks):
        if i + 1 < len(blocks):
            score(i + 1)
        ps = psS[(itbase + i) % nps]
        e = E_sb[(itbase + i) % ne]
        c.act(e[:, :], ps[:, :], AF.Exp, scale=scale)
        m = mask_fn(kb, qc)
        if m is not None:
            c.tt(e[:, :], e[:, :], m, ALU.mult)
        for j in range(4):
            qb = 4 * qc + j
            if kb > qb:
                continue
            c.mm(psO[j][:, :nv + 1], e[:, j * 128:(j + 1) * 128], vaug(kb), start=(kb == 0), stop=(kb == qb))
        if kb == 4 * qc + 3:
            for j in range(4):
                out_cb(qc, j, psO[j])
            qc_done(qc)
    return itbase + len(blocks)


def attn_core_gen(c, qparts, kparts, pv, nv, scale, mask_fn, psS, E_sb, out_cb, qc_done, itbase=0, mask_eng="vector"):
    np_ = len(qparts)
    blocks = [(qc, kb) for qc in range(4) for kb in range(4 * qc + 4)]
    nps, ne = len(psS), len(E_sb)

    def score(i):
        qc, kb = blocks[i]
        ps = psS[(itbase + i) % nps]
        for p in range(np_):
            c.mm(ps[:, :], kparts[p][:, kb * 128:(kb + 1) * 128], qparts[p][:, qc * 512:(qc + 1) * 512],
                 start=(p == 0), stop=(p == np_ - 1))

    score(0)
    for i, (qc, kb) in enumerate(blocks):
        if i + 1 < len(blocks):
            score(i + 1)
        ps = psS[(itbase + i) % nps]
        e = E_sb[(itbase + i) % ne]
        c.act(e[:, :], ps[:, :], AF.Exp, scale=scale)
        m = mask_fn(kb, qc)
        if m is not None:
            me = mask_eng if isinstance(mask_eng, str) else mask_eng[i % len(mask_eng)]
            c.tt(e[:, :], e[:, :], m, ALU.mult, eng=me)
        for j in range(4):
            qb = 4 * qc + j
            if kb > qb:
                continue
            pv(e[:, j * 128:(j + 1) * 128], kb, j, qb)
        if kb == 4 * qc + 3:
            for j in range(4):
                out_cb(qc, j)
            qc_done(qc)
        yield


TOK = 1024
A_TILES = ([(i * 128, 128, False) for i in range(4)] + [(512 + i * 128, 128, False) for i in range(2)]
           + [(768, 64, False), (768, 64, True)]
           + [(832 + i * 128, 128, False) for i in range(12)])
NT_A = len(A_TILES)
R1 = 3072
R2 = 1024


def vdtm_view(buf, row0):
    return View(buf, buf.t.ap()[row0:row0 + 512, :].rearrange("r (two c) -> (r two) c", two=2))


CR1 = 512
CR2 = 256


def s1rows(P, row0, n):
    ch, w = row0 // CR1, row0 % CR1
    assert w + n <= CR1
    b = P["src1"][ch]
    return View(b, b.t.ap()[w:w + n, :])


def g1rows(P, r, row0, n):
    ch, w = row0 // CR1, row0 % CR1
    assert w + n <= CR1
    b = P["G1"][ch]
    return View(b, b.t.ap()[r * CR1 + w:r * CR1 + w + n, :])


def s2rows(P, row0, n):
    ch, w = row0 // CR2, row0 % CR2
    assert w + n <= CR2
    b = P["src2"][ch]
    return View(b, b.t.ap()[w:w + n, :])


def g2rows(P, r, row0, n):
    ch, w = row0 // CR2, row0 % CR2
    assert w + n <= CR2
    b = P["G2"][ch]
    return View(b, b.t.ap()[r * CR2 + w:r * CR2 + w + n, :])


NWB = 5


def phase_A(c, P, l):
    c.begin_phase()
    xsrc = P["xT"] if l == 0 else P["xs"]
    ones, eps_t = P["ones"], P["eps_t"]
    x_sb = c.sb("x_sb", [128, 16, TOK], F32, n=16)
    h_sb = c.sb("h_sb", [128, 16, TOK], BF16, n=16)
    g_sb = c.sb("g_sb", [128, 16], F32)
    sq_tmp = [c.sb(f"sq{i}", [128, 512], BF16) for i in range(3)]
    rstd = c.sb("rstd", [128, 512], F32)
    wblk = [c.sb(f"wblk{i}", [128, 16, 128], BF16) for i in range(NWB)]
    wvd = c.sb("wvd", [128, 16, 512], BF16)
    wrot = c.sb("wrot", [128, 16, 128], BF16)
    osb = [c.sb(f"osb{i}", [128, 512], F32) for i in range(3)]
    ps_ssq = c.ps("ps_ssq", [128, 512])
    pso = [c.ps(f"pso{i}", [128, 512]) for i in range(3)]

    xv = xsrc.ap().rr("(k p) t -> p k t", p=128)
    for k in range(16):
        c.dma(x_sb[k][:, k, :], xv[:, k, :], q="sync")
    c.dma(g_sb[:, :], View(P["gmix"], P["gmix"].t.ap()[l]), q="sync")
    wv = View(P["w_in"], P["w_in"].t.ap()[l].rearrange("(k p) n -> p k n", p=128))

    def load_w(j):
        c0, m, rot = A_TILES[j]
        c.dma(wblk[j % NWB][:, :, :m], wv[:, :, c0:c0 + m], q="gpsimd")

    for j0 in range(NWB - 1):
        load_w(j0)
    for tc in range(2):
        sl = slice(tc * 512, (tc + 1) * 512)
        fm_rmsnorm(c, [x_sb[k][:, k, sl] for k in range(16)], [g_sb[:, k:k + 1] for k in range(16)],
                   [h_sb[k][:, k, sl] for k in range(16)], D, 512, ones, eps_t, ps_ssq, sq_tmp, rstd)
    for q4 in range(4):
        c.dma(wvd[:, 4 * q4:4 * q4 + 4, :], wv[:, 4 * q4:4 * q4 + 4, 2368:2880], q="gpsimd")

    it = 0
    for j in range(NT_A):
        c0, m, rot = A_TILES[j]
        if j + NWB - 1 < NT_A:
            load_w(j + NWB - 1)
        wt = wblk[j % NWB]
        if rot:
            c.memset(wrot[:, :, 64:128], 0.0)
            c.act(wrot[:, :, 0:32], wt[:, :, 32:64], AF.Copy, scale=-1.0)
            c.copy(wrot[:, :, 32:64], wt[:, :, 0:32])
            wt = wrot
        if m < 128:
            m = 128
        for tc in range(2):
            sl = slice(tc * 512, (tc + 1) * 512)
            p = pso[it % 3]
            o = osb[it % 3]
            for k in range(16):
                c.mm(p[:m, :], wt[:, k, :m], h_sb[k][:, k, sl], start=(k == 0), stop=(k == 15))
            c.copy(o[:m, :], p[:m, :], eng=("vector" if it % 2 == 0 else "scalar"))
            c.dma(s1rows(P, j * 128, m)[:, sl], o[:m, :], q="sync")
            it += 1
        if j % 4 == 2 and j >= 6:
            ch = (j - 6) // 4
            c.allgather(P["G1"][ch], P["src1"][ch])
    vdv = vdtm_view(P["src1"][5], 0)
    for tt in range(8):
        p = pso[it % 3]
        o = osb[it % 3]
        for k in range(16):
            c.mm(p[:, :], h_sb[k][:, k, tt * 128:(tt + 1) * 128], wvd[:, k, :], start=(k == 0), stop=(k == 15))
        c.copy(o[:, :], p[:, :], eng=("vector" if it % 2 == 0 else "scalar"))
        c.dma(vdv[tt * 128:(tt + 1) * 128, :], o[:, :], q="sync")
        it += 1
        if tt == 3:
            c.allgather(P["G1"][4], P["src1"][4])
    c.end_phase()


def phase_B(c, P, l):
    c.begin_phase()
    ones, eps_t, ident = P["ones"], P["eps_t"], P["ident"]
    cqf = [c.sb(f"cqf{i}", [128, 4, 512], F32) for i in range(2)]
    ckvf = [c.sb(f"ckvf{i}", [128, 2, 512], F32) for i in range(2)]
    krf = [c.sb(f"krf{i}", [64, 2, 512], F32) for i in range(2)]
    cs_sb = c.sb("cs_sb", [64, 2, S], F32)
    cm_sb = c.sb("cm_sb", [128, 896], BF16)
    cqn = c.sb("cqn", [128, 4, S], BF16, n=4)
    ckvn = c.sb("ckvn", [128, 2, S], BF16, n=2)
    kpe = c.sb("kpe", [64, S], BF16)
    wq = c.sb("wq", [128, 4, 768], BF16)
    wkv = c.sb("wkv", [128, 2, 1024], BF16)
    wrot = c.sb("wrot", [128, 4, 64], BF16)
    g_sb = c.sb("g_sb", [128, 6], F32)
    sq_tmp = [c.sb(f"sq{i}", [128, 512], BF16) for i in range(3)]
    rstd = c.sb("rstd", [128, 512], F32)
    qn = c.sb("qn", [128, S], BF16)
    qpe = c.sb("qpe", [64, S], BF16)
    kn = c.sb("kn", [128, S], BF16)
    vaug = c.sb("vaug", [128, 16, 129], BF16)
    t1 = c.sb("t1", [64, 512], F32)
    t2 = c.sb("t2", [64, 512], F32)
    E_sb = [c.sb(f"E{i}", [128, 512], BF16) for i in range(3)]
    rec = [c.sb(f"rec{i}", [128, 1], F32) for i in range(2)]
    o_sb = [c.sb(f"o_sb{i}", [128, 128], F32) for i in range(2)]
    oT = [c.sb(f"oT{i}", [128, 512], F32) for i in range(2)]
    psS = [c.ps(f"psS{i}", [128, 512]) for i in range(2)]
    psO = [c.ps(f"psO{i}", [128, 512]) for i in range(4)]
    ps_ssq = c.ps("ps_ssq", [128, 512])
    psT = c.ps("psT", [128, 512])

    c.dma(cs_sb[:, :, :], P["cossin"].ap(), q="sync")
    c.dma(cm_sb[:, :], P["cmask"].ap(), q="sync")
    c.dma(g_sb[:, :], View(P["gvec"], P["gvec"].t.ap()[l]), q="sync")
    c.dma(wq[:, :, :], View(P["w_uq"], P["w_uq"].t.ap()[l].rearrange("(k p) n -> p k n", p=128)), q="gpsimd")
    c.dma(wkv[:, :, :], View(P["w_ukv"], P["w_ukv"].t.ap()[l].rearrange("(k p) n -> p k n", p=128)), q="gpsimd")
    c.memset(vaug[:, :, 128:129], 1.0)

    for tc in range(4):
        r, cc = tc // 2, (tc % 2) * 512
        sl = slice(tc * 512, (tc + 1) * 512)
        cq_, ckv_, kr_ = cqf[tc % 2], ckvf[tc % 2], krf[tc % 2]
        c.dma(cq_[:, :, :], g1rows(P, r, 0, 512)[:, cc:cc + 512].rr("(k p) t -> p k t", p=128), q="sync")
        c.dma(ckv_[:, :, :], g1rows(P, r, 512, 256)[:, cc:cc + 512].rr("(k p) t -> p k t", p=128), q="sync")
        c.dma(kr_[:, :, :], g1rows(P, r, 768, 256)[:, cc:cc + 512].rr("(a p) t -> p a t", p=128)[0:64], q="sync")
        fm_rmsnorm(c, [cq_[:, k, :] for k in range(4)], [g_sb[:, k:k + 1] for k in range(4)],
                   [cqn[k][:, k, sl] for k in range(4)], 512, 512, ones, eps_t, ps_ssq, sq_tmp, rstd)
        fm_rmsnorm(c, [ckv_[:, k, :] for k in range(2)], [g_sb[:, 4 + k:5 + k] for k in range(2)],
                   [ckvn[k][:, k, sl] for k in range(2)], 256, 512, ones, eps_t, ps_ssq, sq_tmp, rstd)
        c.tt(t1[:, :], kr_[:, 0, :], cs_sb[:, 0, sl], ALU.mult)
        c.tt(t2[:, :], kr_[:, 1, :], cs_sb[:, 1, sl], ALU.mult)
        c.tt(kpe[:, sl], t1[:, :], t2[:, :], ALU.add)

    it = 0
    scale = 192.0 ** -0.5
    for h in range(4):
        qb0 = h * 192
        c.act(wrot[:, :, 0:32], wq[:, :, qb0 + 160:qb0 + 192], AF.Copy, scale=-1.0)
        c.copy(wrot[:, :, 32:64], wq[:, :, qb0 + 128:qb0 + 160])
        for tc in range(4):
            sl = slice(tc * 512, (tc + 1) * 512)
            p = psS[0]
            for k in range(4):
                c.mm(p[:, :], wq[:, k, qb0:qb0 + 128], cqn[k][:, k, sl], start=(k == 0), stop=(k == 3))
            c.copy(qn[:, sl], p[:, :], eng="scalar")
            p1 = psS[1]
            for k in range(4):
                c.mm(p1[:64, :], wq[:, k, qb0 + 128:qb0 + 192], cqn[k][:, k, sl], start=(k == 0), stop=(k == 3))
            p2 = psO[0]
            for k in range(4):
                c.mm(p2[:64, :], wrot[:, k, :], cqn[k][:, k, sl], start=(k == 0), stop=(k == 3))
            c.tt(t1[:, :], p1[:64, :], cs_sb[:, 0, sl], ALU.mult)
            c.tt(t2[:, :], p2[:64, :], cs_sb[:, 1, sl], ALU.mult)
            c.tt(qpe[:, sl], t1[:, :], t2[:, :], ALU.add)
            p3 = psO[1]
            for k in range(2):
                c.mm(p3[:, :], wkv[:, k, h * 256:h * 256 + 128], ckvn[k][:, k, sl], start=(k == 0), stop=(k == 1))
            c.copy(kn[:, sl], p3[:, :], eng="vector")
        for kq in range(4):
            p = psO[2 + kq % 2]
            for j in range(4):
                kb = kq * 4 + j
                for k in range(2):
                    c.mm(p[:, j * 128:(j + 1) * 128], ckvn[k][:, k, kb * 128:(kb + 1) * 128],
                         wkv[:, k, h * 256 + 128:h * 256 + 256], start=(k == 0), stop=(k == 1))
            c.copy(vaug[:, kq * 4:(kq + 1) * 4, 0:128], p[:, :].rr("p (j d) -> p j d", j=4), eng="scalar")

        def mask_fn(kb, qc):
            c0 = 512 * qc - 128 * kb
            if c0 >= 128:
                return None
            return cm_sb[:, c0 + 384:c0 + 384 + 512]

        def out_cb(qc, j, po):
            r_ = rec[j % 2]
            o = o_sb[j % 2]
            c.recip(r_[:, :], po[:, 128:129])
            c.ts(o[:, :], po[:, 0:128], r_[:, 0:1], ALU.mult)
            c.transpose(psT[:, j * 128:(j + 1) * 128], o[:, :], ident[:, :])

        def qc_done(qc, h=h):
            ot = oT[qc % 2]
            c.copy(ot[:, :], psT[:, :], eng="vector")
            c.dma(s2rows(P, h * 128, 128)[:, qc * 512:(qc + 1) * 512], ot[:, :], q="sync")

        it = attn_core(c, [qn, qpe], [kn, kpe], lambda kb: vaug[:, kb, :], 128, scale, mask_fn, psS, psO, E_sb,
                       out_cb, qc_done, it)
    c.end_phase()


MAGIC = 12582912.0
TWO_PI = 6.283185307179586
CW1 = 6.28125
CW2 = TWO_PI - CW1
PI_LO = 3.1415925
CH = 256


def phase_C(c, P, l):
    c.begin_phase()
    m_sb = P["m_sb"]
    u_f = c.sb("u_f", [128, 2, S], F32)
    u_t = c.sb("u_t", [128, 2, S], F32)
    u_b = c.sb("u_b", [128, 2, S], BF16)
    p_sb = c.sb("p_sb", [128, 3, 8], F32)
    B_sb = c.sb("B_sb", [128, 2, 8, 128], BF16)
    C_sb = c.sb("C_sb", [128, 2, 8, 128], BF16)
    nC_sb = c.sb("nC_sb", [128, 2, 8, 128], BF16)
    d_sb = c.sb("d_sb", [128, 2], F32)
    j_sb = c.sb("j_sb", [128, 512], F32)
    for r in range(2):
        c.dma(u_f[:, :, r * TOK:(r + 1) * TOK], g1rows(P, r, 1024, 256).rr("(k p) t -> p k t", p=128), q="sync")
        c.dma(u_t[:, :, r * TOK:(r + 1) * TOK], g1rows(P, r, 1280, 256).rr("(k p) t -> p k t", p=128), q="sync")
    c.dma(p_sb[:, :, :], View(P["prm"], P["prm"].t.ap()[l]), q="sync")
    c.dma(B_sb[:, :, :, :].rr("p a k m -> p (a k m)"), View(P["Bblk"], P["Bblk"].t.ap()[l].rearrange("p a k m -> p (a k m)")), q="gpsimd")
    c.dma(C_sb[:, :, :, :].rr("p a k m -> p (a k m)"), View(P["Cblk"], P["Cblk"].t.ap()[l].rearrange("p a k m -> p (a k m)")), q="gpsimd")
    c.dma(d_sb[:, :], View(P["dsk"], P["dsk"].t.ap()[l]), q="sync")
    c.dma(j_sb[:, :], P["jrow"].ap(), q="sync")
    for ut in range(2):
        c.ts(u_f[:, ut, :], u_f[:, ut, :], m_sb[:, 0:1], ALU.mult)
        c.stt(u_f[:, ut, :], u_t[:, ut, :], m_sb[:, 1:2], u_f[:, ut, :], ALU.mult, ALU.add)
        c.copy(u_b[:, ut, :], u_f[:, ut, :], eng="scalar")
    c.act(nC_sb[:, :, :, :].rr("p a k m -> p (a k m)"), C_sb[:, :, :, :].rr("p a k m -> p (a k m)"), AF.Copy, scale=-1.0)

    def small(name, n=8):
        return c.sb(name, [128, n], F32)

    kt = c.sb("kt", [128, 512], F32)
    red = c.sb("red", [128, 512], F32)
    ang = c.sb("ang", [128, 512], F32)

    def sin_of(out, a, n, shift=0.0):
        src = a
        if shift != 0.0:
            c.ts(ang[:, :n], a, shift, ALU.add)
            src = ang[:, :n]
        c.ts(kt[:, :n], src, 1.0 / TWO_PI, ALU.mult, MAGIC, ALU.add)
        c.ts(kt[:, :n], kt[:, :n], -MAGIC, ALU.add)
        c.stt(red[:, :n], kt[:, :n], -CW1, src, ALU.mult, ALU.add)
        c.stt(red[:, :n], kt[:, :n], -CW2, red[:, :n], ALU.mult, ALU.add)
        c.ts(red[:, :n], red[:, :n], -PI_LO, ALU.max, PI_LO, ALU.min)
        c.act(out, red[:, :n], AF.Sin)

    lre, dt, ldr, th, rr_ = small("lre"), small("dt"), small("ldr"), small("th"), small("rr")
    cth, sth, ar, ai, arm1 = small("cth"), small("sth"), small("ar"), small("ai"), small("arm1")
    nr, ni, den, tmp, cr, ci = small("nr"), small("ni"), small("den"), small("tmp"), small("cr"), small("ci")
    thT, cT, sT, nsT = small("thT"), small("cT"), small("sT"), small("nsT")
    are, aim, ldt = p_sb[:, 0, :], p_sb[:, 1, :], p_sb[:, 2, :]
    c.ts(lre[:, :], are, -1e-4, ALU.min)
    c.act(dt[:, :], ldt, AF.Exp)
    c.tt(ldr[:, :], lre[:, :], dt[:, :], ALU.mult)
    c.tt(th[:, :], aim, dt[:, :], ALU.mult)
    c.act(rr_[:, :], ldr[:, :], AF.Exp)
    sin_of(sth[:, :], th[:, :], 8)
    sin_of(cth[:, :], th[:, :], 8, shift=np.pi / 2)
    c.tt(ar[:, :], rr_[:, :], cth[:, :], ALU.mult)
    c.tt(ai[:, :], rr_[:, :], sth[:, :], ALU.mult)
    c.ts(arm1[:, :], ar[:, :], -1.0, ALU.add)
    c.tt(nr[:, :], arm1[:, :], lre[:, :], ALU.mult)
    c.tt(tmp[:, :], ai[:, :], aim, ALU.mult)
    c.tt(nr[:, :], nr[:, :], tmp[:, :], ALU.add)
    c.tt(ni[:, :], ai[:, :], lre[:, :], ALU.mult)
    c.tt(tmp[:, :], arm1[:, :], aim, ALU.mult)
    c.tt(ni[:, :], ni[:, :], tmp[:, :], ALU.subtract)
    c.tt(den[:, :], lre[:, :], lre[:, :], ALU.mult)
    c.tt(tmp[:, :], aim, aim, ALU.mult)
    c.tt(den[:, :], den[:, :], tmp[:, :], ALU.add)
    c.recip(den[:, :], den[:, :])
    c.tt(cr[:, :], nr[:, :], den[:, :], ALU.mult)
    c.tt(ci[:, :], ni[:, :], den[:, :], ALU.mult)
    c.ts(thT[:, :], th[:, :], float(CH), ALU.mult)
    sin_of(sT[:, :], thT[:, :], 8)
    sin_of(cT[:, :], thT[:, :], 8, shift=np.pi / 2)
    c.ts(nsT[:, :], sT[:, :], -1.0, ALU.mult)

    Cc = c.sb("Cc", [128, 8, 512], F32, n=8)
    Sn = c.sb("Sn", [128, 8, 512], F32, n=8)
    Tr = c.sb("Tr", [128, 8, 512], F32, n=8)
    Ti = c.sb("Ti", [128, 8, 512], F32, n=8)
    tang = c.sb("tang", [128, 512], F32)
    ttmp = c.sb("ttmp", [128, 512], F32)
    for k in range(8):
        c.ts(tang[:, :], j_sb[:, :], th[:, k:k + 1], ALU.mult)
        sin_of(Sn[k][:, k, :], tang[:, :], 512)
        sin_of(Cc[k][:, k, :], tang[:, :], 512, shift=np.pi / 2)
        c.ts(ttmp[:, :], Cc[k][:, k, :], cr[:, k:k + 1], ALU.mult)
        c.stt(Tr[k][:, k, :], Sn[k][:, k, :], ci[:, k:k + 1], ttmp[:, :], ALU.mult, ALU.add)
        c.ts(ttmp[:, :], Sn[k][:, k, :], cr[:, k:k + 1], ALU.mult)
        c.stt(Ti[k][:, k, :], Cc[k][:, k, :], ci[:, k:k + 1], ttmp[:, :], ALU.mult, ALU.subtract)

    NBUF = 2
    t1 = [c.sb(f"t1_{i}", [128, 512], F32) for i in range(NBUF)]
    t2 = [c.sb(f"t2_{i}", [128, 512], F32) for i in range(NBUF)]
    t3 = [c.sb(f"t3_{i}", [128, 512], F32) for i in range(NBUF)]
    t4 = [c.sb(f"t4_{i}", [128, 512], F32) for i in range(NBUF)]
    xr = [c.sb(f"xr_{i}", [128, 512], F32) for i in range(NBUF)]
    xi = [c.sb(f"xi_{i}", [128, 512], F32) for i in range(NBUF)]
    gr = [c.sb(f"gr_{i}", [128, 512], F32) for i in range(NBUF)]
    gi = [c.sb(f"gi_{i}", [128, 512], F32) for i in range(NBUF)]
    Pp = [[c.sb(f"P{j}_{i}", [128, 512], BF16) for i in range(NBUF)] for j in range(4)]
    init = [c.sb(f"init{k}", [128, 2], F32) for k in range(8)]
    itmp = c.sb("itmp", [128, 2], F32)
    ysb = [c.sb(f"ysb{i}", [128, 512], F32) for i in range(2)]
    gsb = [c.sb(f"gsb{i}", [128, 512], F32) for i in range(2)]
    psx = [c.ps(f"psx{i}", [128, 512]) for i in range(4)]
    psy = [c.ps(f"psy{i}", [128, 512]) for i in range(2)]

    it = 0
    for ut in range(2):
        for tb in range(4):
            sl = slice(tb * 512, (tb + 1) * 512)
            py = psy[(ut * 4 + tb) % 2]
            for kk in range(4):
                k = ut * 4 + kk
                b = it % NBUF
                pr, pi_ = psx[(2 * it) % 4], psx[(2 * it + 1) % 4]
                c.mm(pr[:, :], B_sb[:, 0, k, :], u_b[:, ut, sl])
                c.mm(pi_[:, :], B_sb[:, 1, k, :], u_b[:, ut, sl])
                c.tt(t1[b][:, :], pr[:, :], Tr[k][:, k, :], ALU.mult)
                c.tt(t2[b][:, :], pi_[:, :], Ti[k][:, k, :], ALU.mult)
                c.tt(t3[b][:, :], pi_[:, :], Tr[k][:, k, :], ALU.mult)
                c.tt(t4[b][:, :], pr[:, :], Ti[k][:, k, :], ALU.mult)
                c.tt(xr[b][:, :], t1[b][:, :], t2[b][:, :], ALU.subtract, eng="gpsimd")
                c.tt(xi[b][:, :], t3[b][:, :], t4[b][:, :], ALU.add, eng="gpsimd")
                for sub in range(2):
                    ss = slice(sub * CH, (sub + 1) * CH)
                    first = (tb == 0 and sub == 0)
                    i_r = 0.0 if first else init[k][:, 0:1]
                    i_i = 0.0 if first else init[k][:, 1:2]
                    c.scan(gr[b][:, ss], rr_[:, k:k + 1].bcast([128, CH]), xr[b][:, ss], i_r)
                    c.scan(gi[b][:, ss], rr_[:, k:k + 1].bcast([128, CH]), xi[b][:, ss], i_i)
                    if not (tb == 3 and sub == 1):
                        fr = gr[b][:, (sub + 1) * CH - 1:(sub + 1) * CH]
                        fi = gi[b][:, (sub + 1) * CH - 1:(sub + 1) * CH]
                        c.ts(itmp[:, 0:1], fr, cT[:, k:k + 1], ALU.mult)
                        c.ts(itmp[:, 1:2], fr, sT[:, k:k + 1], ALU.mult)
                        c.stt(init[k][:, 0:1], fi, nsT[:, k:k + 1], itmp[:, 0:1], ALU.mult, ALU.add)
                        c.stt(init[k][:, 1:2], fi, cT[:, k:k + 1], itmp[:, 1:2], ALU.mult, ALU.add)
                c.tt(Pp[0][b][:, :], Cc[k][:, k, :], gr[b][:, :], ALU.mult, eng="gpsimd")
                c.tt(Pp[1][b][:, :], Sn[k][:, k, :], gi[b][:, :], ALU.mult, eng="gpsimd")
                c.tt(Pp[2][b][:, :], Sn[k][:, k, :], gr[b][:, :], ALU.mult)
                c.tt(Pp[3][b][:, :], Cc[k][:, k, :], gi[b][:, :], ALU.mult)
                c.mm(py[:, :], C_sb[:, 0, k, :], Pp[0][b][:, :], start=(kk == 0), stop=False)
                c.mm(py[:, :], nC_sb[:, 0, k, :], Pp[1][b][:, :], start=False, stop=False)
                c.mm(py[:, :], nC_sb[:, 1, k, :], Pp[2][b][:, :], start=False, stop=False)
                c.mm(py[:, :], nC_sb[:, 1, k, :], Pp[3][b][:, :], start=False, stop=(kk == 3))
                it += 1
            yb = ysb[(ut * 4 + tb) % 2]
            gb = gsb[(ut * 4 + tb) % 2]
            c.stt(yb[:, :], u_f[:, ut, sl], d_sb[:, ut:ut + 1], py[:, :], ALU.mult, ALU.add)
            c.act(gb[:, :], yb[:, :], AF.Gelu_apprx_tanh)
            c.dma(s2rows(P, 512 + ut * 128, 128)[:, sl], gb[:, :], q="sync")
    c.end_phase()


def phase_D(c, P, l):
    c.begin_phase()
    m_sb, ident = P["m_sb"], P["ident"]
    qc_ = [c.sb(f"qc{i}", [128, 2, S], F32) for i in range(2)]
    kc_ = [c.sb(f"kc{i}", [128, 2, S], F32) for i in range(2)]
    vc_ = [c.sb(f"vc{i}", [128, 16, 256], F32) for i in range(2)]
    q_bf = c.sb("q_bf", [128, 2, S], BF16)
    k_bf = c.sb("k_bf", [128, 2, S], BF16)
    vaug = c.sb("vaug", [128, 16, 4, 65], BF16)
    dm_sb = c.sb("dm_sb", [128, 2432], BF16)
    E_sb = [c.sb(f"E{i}", [128, 512], BF16) for i in range(3)]
    rec = [c.sb(f"rec{i}", [128, 1], F32) for i in range(2)]
    o_sb = [c.sb(f"o_sb{i}", [128, 64], F32) for i in range(2)]
    oT = [c.sb(f"oT{i}", [64, 512], F32) for i in range(2)]
    psS = [c.ps(f"psS{i}", [128, 512]) for i in range(2)]
    psO = [c.ps(f"psO{i}", [128, 512]) for i in range(4)]
    psT = c.ps("psT", [128, 512])

    c.dma(dm_sb[:, :], P["dmask"].ap(), q="sync")
    for r in range(2):
        for s_ in range(2):
            c.dma(qc_[s_][:, :, r * TOK:(r + 1) * TOK], g1rows(P, r, 1536 + s_ * 256, 256).rr("(k p) t -> p k t", p=128), q="sync")
            c.dma(kc_[s_][:, :, r * TOK:(r + 1) * TOK], g1rows(P, r, 2048 + s_ * 256, 256).rr("(k p) t -> p k t", p=128), q="sync")
            vv = vdtm_view(P["G1"][5], r * CR1)
            c.dma(vc_[s_][:, r * 8:(r + 1) * 8, :],
                  View(P["G1"][5], vv.ap.rearrange("(kb p) n -> p kb n", p=128)[:, :, s_ * 256:(s_ + 1) * 256]), q="sync")
    c.memset(vaug[:, :, :, 64:65], 1.0)
    for t in range(2):
        c.ts(qc_[0][:, t, :], qc_[0][:, t, :], m_sb[:, 0:1], ALU.mult)
        c.stt(q_bf[:, t, :], qc_[1][:, t, :], m_sb[:, 1:2], qc_[0][:, t, :], ALU.mult, ALU.add)
        c.ts(kc_[0][:, t, :], kc_[0][:, t, :], m_sb[:, 0:1], ALU.mult)
        c.stt(k_bf[:, t, :], kc_[1][:, t, :], m_sb[:, 1:2], kc_[0][:, t, :], ALU.mult, ALU.add)
    v0 = vc_[0][:, :, :].rr("p k n -> p (k n)")
    v1 = vc_[1][:, :, :].rr("p k n -> p (k n)")
    c.ts(v0, v0, m_sb[:, 0:1], ALU.mult)
    c.stt(v0, v1, m_sb[:, 1:2], v0, ALU.mult, ALU.add)
    for h in range(4):
        c.copy(vaug[:, :, h, 0:64], vc_[0][:, :, h * 64:(h + 1) * 64], eng="scalar")

    it = 0
    for h in range(4):
        t, r0 = h // 2, (h % 2) * 64

        def mask_fn(kb, qc):
            c0 = 512 * qc - 128 * kb
            return dm_sb[:, c0 + 384:c0 + 384 + 512]

        def out_cb(qc, j, po):
            r_ = rec[j % 2]
            o = o_sb[j % 2]
            c.recip(r_[:, :], po[:, 64:65])
            c.ts(o[:, :], po[:, 0:64], r_[:, 0:1], ALU.mult)
            c.transpose(psT[:64, j * 128:(j + 1) * 128], o[:, :], ident[:, :])

        def qc_done(qc, h=h):
            ot = oT[qc % 2]
            c.copy(ot[:, :], psT[:64, :], eng="vector")
            c.dma(s2rows(P, 768 + h * 64, 64)[:, qc * 512:(qc + 1) * 512], ot[:, :], q="sync")

        it = attn_core(c, [q_bf[r0:r0 + 64, t, :]], [k_bf[r0:r0 + 64, t, :]], lambda kb, h=h: vaug[:, kb, h, :],
                       64, 0.125, mask_fn, psS, psO, E_sb, out_cb, qc_done, it)
    c.end_phase()


def phase_CD(c, P, l):
    c.begin_phase()
    m_sb, ident = P["m_sb"], P["ident"]
    q_bf = c.sb("q_bf", [128, 2, S], BF16)
    k_bf = c.sb("k_bf", [128, 2, S], BF16)
    vaug = c.sb("vaug", [128, 16, 4, 65], BF16)
    dm_sb = c.sb("dm_sb", [128, 2432], BF16)
    E_sb = [c.sb(f"E{i}", [128, 512], BF16) for i in range(3)]
    rec = [c.sb(f"rec{i}", [128, 1], F32) for i in range(2)]
    o_sb = [c.sb(f"o_sb{i}", [128, 64], F32) for i in range(2)]
    oT = [c.sb(f"oT{i}", [64, 512], F32) for i in range(2)]
    psS = [c.ps(f"psS{i}", [128, 512]) for i in range(2)]
    psO = c.ps("psO", [128, 512])
    psT = c.ps("psT", [128, 512])
    psx = [c.ps(f"psx{i}", [128, 512]) for i in range(2)]
    psy = [c.ps(f"psy{i}", [128, 512]) for i in range(2)]
    c.dma(dm_sb[:, :], P["dmask"].ap(), q="sync")
    c.memset(vaug[:, :, :, 64:65], 1.0)
    c.push_scope()
    qk = [c.sb(f"qk{i}", [128, 2, TOK], F32) for i in range(2)]
    vc_ = [c.sb(f"vc{i}", [128, 8, 256], F32) for i in range(2)]
    for r in range(2):
        ts_ = slice(r * TOK, (r + 1) * TOK)
        for (row0, dst) in ((1536, q_bf), (2048, k_bf)):
            for s_ in range(2):
                c.dma(qk[s_][:, :, :], g1rows(P, r, row0 + s_ * 256, 256).rr("(k p) t -> p k t", p=128), q="sync")
            for t in range(2):
                c.ts(qk[0][:, t, :], qk[0][:, t, :], m_sb[:, 0:1], ALU.mult)
                c.stt(dst[:, t, ts_], qk[1][:, t, :], m_sb[:, 1:2], qk[0][:, t, :], ALU.mult, ALU.add)
        vv = vdtm_view(P["G1"][5], r * CR1)
        for s_ in range(2):
            c.dma(vc_[s_][:, :, :], View(P["G1"][5], vv.ap.rearrange("(kb p) n -> p kb n", p=128)[:, :, s_ * 256:(s_ + 1) * 256]), q="sync")
        v0 = vc_[0][:, :, :].rr("p k n -> p (k n)")
        v1 = vc_[1][:, :, :].rr("p k n -> p (k n)")
        c.ts(v0, v0, m_sb[:, 0:1], ALU.mult)
        c.stt(v0, v1, m_sb[:, 1:2], v0, ALU.mult, ALU.add)
        for h in range(4):
            c.copy(vaug[:, r * 8:(r + 1) * 8, h, 0:64], vc_[0][:, :, h * 64:(h + 1) * 64], eng="scalar")
    c.pop_scope()

    u_f = c.sb("u_f", [128, 2, S], F32)
    u_b = c.sb("u_b", [128, 2, S], BF16)
    B_sb = c.sb("B_sb", [128, 2, 8, 128], BF16)
    C_sb = c.sb("C_sb", [128, 2, 8, 128], BF16)
    nC_sb = c.sb("nC_sb", [128, 2, 8, 128], BF16)
    d_sb = c.sb("d_sb", [128, 2], F32)

    def small(name, n=8):
        return c.sb(name, [128, n], F32)

    rr_, cT, sT, nsT = small("rr"), small("cT"), small("sT"), small("nsT")
    Cc = c.sb("Cc", [128, 8, CH], F32, n=8)
    Sn = c.sb("Sn", [128, 8, CH], F32, n=8)
    Tr = c.sb("Tr", [128, 8, CH], F32, n=8)
    Ti = c.sb("Ti", [128, 8, CH], F32, n=8)

    def tab2(bufs, k):
        return View(bufs[k], bufs[k].t[:, k, :].unsqueeze(1).broadcast_to([128, 2, CH]))

    def v3(view):
        return view.rr("p (a j) -> p a j", a=2)

    c.push_scope()
    u_t = c.sb("u_t", [128, 2, S], F32)
    p_sb = c.sb("p_sb", [128, 3, 8], F32)
    j_sb = c.sb("j_sb", [128, 512], F32)
    for r in range(2):
        c.dma(u_f[:, :, r * TOK:(r + 1) * TOK], g1rows(P, r, 1024, 256).rr("(k p) t -> p k t", p=128), q="sync")
        c.dma(u_t[:, :, r * TOK:(r + 1) * TOK], g1rows(P, r, 1280, 256).rr("(k p) t -> p k t", p=128), q="sync")
    c.dma(p_sb[:, :, :], View(P["prm"], P["prm"].t.ap()[l]), q="sync")
    c.dma(B_sb[:, :, :, :].rr("p a k m -> p (a k m)"), View(P["Bblk"], P["Bblk"].t.ap()[l].rearrange("p a k m -> p (a k m)")), q="gpsimd")
    c.dma(C_sb[:, :, :, :].rr("p a k m -> p (a k m)"), View(P["Cblk"], P["Cblk"].t.ap()[l].rearrange("p a k m -> p (a k m)")), q="gpsimd")
    c.dma(d_sb[:, :], View(P["dsk"], P["dsk"].t.ap()[l]), q="sync")
    c.dma(j_sb[:, :], P["jrow"].ap(), q="sync")
    for ut in range(2):
        c.ts(u_f[:, ut, :], u_f[:, ut, :], m_sb[:, 0:1], ALU.mult)
        c.stt(u_f[:, ut, :], u_t[:, ut, :], m_sb[:, 1:2], u_f[:, ut, :], ALU.mult, ALU.add)
        c.copy(u_b[:, ut, :], u_f[:, ut, :], eng="scalar")
    c.act(nC_sb[:, :, :, :].rr("p a k m -> p (a k m)"), C_sb[:, :, :, :].rr("p a k m -> p (a k m)"), AF.Copy, scale=-1.0)
    kt = c.sb("kt", [128, CH], F32)
    red = c.sb("red", [128, CH], F32)
    ang = c.sb("ang", [128, CH], F32)

    def sin_of(out, a, n, shift=0.0):
        src = a
        if shift != 0.0:
            c.ts(ang[:, :n], a, shift, ALU.add)
            src = ang[:, :n]
        c.ts(kt[:, :n], src, 1.0 / TWO_PI, ALU.mult, MAGIC, ALU.add)
        c.ts(kt[:, :n], kt[:, :n], -MAGIC, ALU.add)
        c.stt(red[:, :n], kt[:, :n], -CW1, src, ALU.mult, ALU.add)
        c.stt(red[:, :n], kt[:, :n], -CW2, red[:, :n], ALU.mult, ALU.add)
        c.ts(red[:, :n], red[:, :n], -PI_LO, ALU.max, PI_LO, ALU.min)
        c.act(out, red[:, :n], AF.Sin)

    lre, dt, ldr, th = small("lre"), small("dt"), small("ldr"), small("th")
    cth, sth, ar, ai, arm1 = small("cth"), small("sth"), small("ar"), small("ai"), small("arm1")
    nr, ni, den, tmp, cr, ci = small("nr"), small("ni"), small("den"), small("tmp"), small("cr"), small("ci")
    thT = small("thT")
    are, aim, ldt = p_sb[:, 0, :], p_sb[:, 1, :], p_sb[:, 2, :]
    c.ts(lre[:, :], are, -1e-4, ALU.min)
    c.act(dt[:, :], ldt, AF.Exp)
    c.tt(ldr[:, :], lre[:, :], dt[:, :], ALU.mult)
    c.tt(th[:, :], aim, dt[:, :], ALU.mult)
    c.act(rr_[:, :], ldr[:, :], AF.Exp)
    sin_of(sth[:, :], th[:, :], 8)
    sin_of(cth[:, :], th[:, :], 8, shift=np.pi / 2)
    c.tt(ar[:, :], rr_[:, :], cth[:, :], ALU.mult)
    c.tt(ai[:, :], rr_[:, :], sth[:, :], ALU.mult)
    c.ts(arm1[:, :], ar[:, :], -1.0, ALU.add)
    c.tt(nr[:, :], arm1[:, :], lre[:, :], ALU.mult)
    c.tt(tmp[:, :], ai[:, :], aim, ALU.mult)
    c.tt(nr[:, :], nr[:, :], tmp[:, :], ALU.add)
    c.tt(ni[:, :], ai[:, :], lre[:, :], ALU.mult)
    c.tt(tmp[:, :], arm1[:, :], aim, ALU.mult)
    c.tt(ni[:, :], ni[:, :], tmp[:, :], ALU.subtract)
    c.tt(den[:, :], lre[:, :], lre[:, :], ALU.mult)
    c.tt(tmp[:, :], aim, aim, ALU.mult)
    c.tt(den[:, :], den[:, :], tmp[:, :], ALU.add)
    c.recip(den[:, :], den[:, :])
    c.tt(cr[:, :], nr[:, :], den[:, :], ALU.mult)
    c.tt(ci[:, :], ni[:, :], den[:, :], ALU.mult)
    c.ts(thT[:, :], th[:, :], float(CH), ALU.mult)
    sin_of(sT[:, :], thT[:, :], 8)
    sin_of(cT[:, :], thT[:, :], 8, shift=np.pi / 2)
    c.ts(nsT[:, :], sT[:, :], -1.0, ALU.mult)
    tang = c.sb("tang", [128, CH], F32)
    ttmp = c.sb("ttmp", [128, CH], F32)
    for k in range(8):
        c.ts(tang[:, :], j_sb[:, 0:CH], th[:, k:k + 1], ALU.mult)
        sin_of(Sn[k][:, k, :], tang[:, :], CH)
        sin_of(Cc[k][:, k, :], tang[:, :], CH, shift=np.pi / 2)
        c.ts(ttmp[:, :], Cc[k][:, k, :], cr[:, k:k + 1], ALU.mult)
        c.stt(Tr[k][:, k, :], Sn[k][:, k, :], ci[:, k:k + 1], ttmp[:, :], ALU.mult, ALU.add)
        c.ts(ttmp[:, :], Sn[k][:, k, :], cr[:, k:k + 1], ALU.mult)
        c.stt(Ti[k][:, k, :], Cc[k][:, k, :], ci[:, k:k + 1], ttmp[:, :], ALU.mult, ALU.subtract)
    c.pop_scope()

    NBUF = 3
    t1 = [c.sb(f"t1_{i}", [128, 512], F32) for i in range(NBUF)]
    t2 = [c.sb(f"t2_{i}", [128, 512], F32) for i in range(NBUF)]
    t3 = [c.sb(f"t3_{i}", [128, 512], F32) for i in range(NBUF)]
    t4 = [c.sb(f"t4_{i}", [128, 512], F32) for i in range(NBUF)]
    gr = [c.sb(f"gr_{i}", [128, 512], F32) for i in range(NBUF)]
    gi = [c.sb(f"gi_{i}", [128, 512], F32) for i in range(NBUF)]
    Pp = [[c.sb(f"P{j}_{i}", [128, 512], BF16) for i in range(2)] for j in range(4)]
    init = [c.sb(f"init{k}", [128, 2], F32) for k in range(8)]
    itmp = [c.sb(f"itmp{i}", [128, 2], F32) for i in range(2)]
    ysb = [c.sb(f"ysb{i}", [128, 512], F32) for i in range(2)]
    gsb = [c.sb(f"gsb{i}", [128, 512], F32) for i in range(2)]

    units = [(ut, tb, kk) for ut in range(2) for tb in range(4) for kk in range(4)]

    def stage1(u):
        ut, tb, kk = units[u]
        k = ut * 4 + kk
        b = u % NBUF
        sl = slice(tb * 512, (tb + 1) * 512)
        pr, pi_ = psx[0], psx[1]
        c.mm(pr[:, :], B_sb[:, 0, k, :], u_b[:, ut, sl])
        c.mm(pi_[:, :], B_sb[:, 1, k, :], u_b[:, ut, sl])
        c.tt(v3(t1[b][:, :]), v3(pr[:, :]), tab2(Tr, k), ALU.mult)
        c.tt(v3(t2[b][:, :]), v3(pi_[:, :]), tab2(Ti, k), ALU.mult)
        c.tt(v3(t3[b][:, :]), v3(pi_[:, :]), tab2(Tr, k), ALU.mult)
        c.tt(v3(t4[b][:, :]), v3(pr[:, :]), tab2(Ti, k), ALU.mult)
        c.tt(t1[b][:, :], t1[b][:, :], t2[b][:, :], ALU.subtract)
        c.tt(t3[b][:, :], t3[b][:, :], t4[b][:, :], ALU.add)

    def stage2(u):
        ut, tb, kk = units[u]
        k = ut * 4 + kk
        b = u % NBUF
        for sub in range(2):
            ss = slice(sub * CH, (sub + 1) * CH)
            first = (tb == 0 and sub == 0)
            i_r = 0.0 if first else init[k][:, 0:1]
            i_i = 0.0 if first else init[k][:, 1:2]
            c.scan(gr[b][:, ss], rr_[:, k:k + 1].bcast([128, CH]), t1[b][:, ss], i_r)
            c.scan(gi[b][:, ss], rr_[:, k:k + 1].bcast([128, CH]), t3[b][:, ss], i_i)
            if not (tb == 3 and sub == 1):
                fr = gr[b][:, (sub + 1) * CH - 1:(sub + 1) * CH]
                fi = gi[b][:, (sub + 1) * CH - 1:(sub + 1) * CH]
                tm = itmp[sub]
                c.ts(tm[:, 0:1], fr, cT[:, k:k + 1], ALU.mult)
                c.ts(tm[:, 1:2], fr, sT[:, k:k + 1], ALU.mult)
                c.stt(init[k][:, 0:1], fi, nsT[:, k:k + 1], tm[:, 0:1], ALU.mult, ALU.add)
                c.stt(init[k][:, 1:2], fi, cT[:, k:k + 1], tm[:, 1:2], ALU.mult, ALU.add)

    def stage3(u):
        ut, tb, kk = units[u]
        k = ut * 4 + kk
        b = u % NBUF
        pb = u % 2
        sl = slice(tb * 512, (tb + 1) * 512)
        py = psy[(ut * 4 + tb) % 2]
        c.tt(v3(Pp[0][pb][:, :]), tab2(Cc, k), v3(gr[b][:, :]), ALU.mult)
        c.tt(v3(Pp[1][pb][:, :]), tab2(Sn, k), v3(gi[b][:, :]), ALU.mult)
        c.tt(v3(Pp[2][pb][:, :]), tab2(Sn, k), v3(gr[b][:, :]), ALU.mult)
        c.tt(v3(Pp[3][pb][:, :]), tab2(Cc, k), v3(gi[b][:, :]), ALU.mult)
        c.mm(py[:, :], C_sb[:, 0, k, :], Pp[0][pb][:, :], start=(kk == 0), stop=False)
        c.mm(py[:, :], nC_sb[:, 0, k, :], Pp[1][pb][:, :], start=False, stop=False)
        c.mm(py[:, :], nC_sb[:, 1, k, :], Pp[2][pb][:, :], start=False, stop=False)
        c.mm(py[:, :], nC_sb[:, 1, k, :], Pp[3][pb][:, :], start=False, stop=(kk == 3))
        if kk == 3:
            yb = ysb[(ut * 4 + tb) % 2]
            gb = gsb[(ut * 4 + tb) % 2]
            c.stt(yb[:, :], u_f[:, ut, sl], d_sb[:, ut:ut + 1], py[:, :], ALU.mult, ALU.add)
            c.act(gb[:, :], yb[:, :], AF.Gelu_apprx_tanh)
            c.dma(s2rows(P, 512 + ut * 128, 128)[:, sl], gb[:, :], q="sync")

    def c_main():
        n = len(units)
        for s_ in range(n + 2):
            if s_ < n:
                stage1(s_)
            if 0 <= s_ - 1 < n:
                stage2(s_ - 1)
            if 0 <= s_ - 2 < n:
                stage3(s_ - 2)
            yield

    def d_main():
        it = 0
        for h in range(4):
            t, r0 = h // 2, (h % 2) * 64

            def mask_fn(kb, qc):
                c0 = 512 * qc - 128 * kb
                return dm_sb[:, c0 + 384:c0 + 384 + 512]

            def pv(e_view, kb, j, qb, h=h):
                c.mm(psO[:, j * 65:(j + 1) * 65], e_view, vaug[:, kb, h, :], start=(kb == 0 and j == 0), stop=(kb == qb),
                     skip_group_check=True)

            def out_cb(qc, j):
                r_ = rec[j % 2]
                o = o_sb[j % 2]
                c.recip(r_[:, :], psO[:, j * 65 + 64:j * 65 + 65])
                c.ts(o[:, :], psO[:, j * 65:j * 65 + 64], r_[:, 0:1], ALU.mult)
                c.transpose(psT[:64, j * 128:(j + 1) * 128], o[:, :], ident[:, :])

            def qc_done(qc, h=h):
                ot = oT[qc % 2]
                c.copy(ot[:, :], psT[:64, :], eng="vector")
                c.dma(s2rows(P, 768 + h * 64, 64)[:, qc * 512:(qc + 1) * 512], ot[:, :], q="sync")

            yield from attn_core_gen(c, [q_bf[r0:r0 + 64, t, :]], [k_bf[r0:r0 + 64, t, :]], pv, 64, 0.125, mask_fn,
                                     psS, E_sb, out_cb, qc_done, it, mask_eng="vector")
            it += 40

    cg, dg = c_main(), d_main()
    c_left, d_left = True, True
    while c_left or d_left:
        if c_left:
            try:
                next(cg)
            except StopIteration:
                c_left = False
        for _ in range(5):
            if d_left:
                try:
                    next(dg)
                except StopIteration:
                    d_left = False
    c.end_phase()


NF = DFF // 128
FPG = 4
FG = NF // FPG


def e_src_rows(k):
    if k < 8:
        return (k // 4), (k % 4) * 128
    if k < 12:
        return ((k - 8) // 2), 512 + ((k - 8) % 2) * 128
    return ((k - 12) // 2), 768 + ((k - 12) % 2) * 128


def phase_E(c, P, l):
    final = (l == P["nlayers"] - 1)
    c.begin_phase()
    m_sb = P["m_sb"]
    ones, eps_t = P["ones"], P["eps_t"]
    xsrc = P["xT"] if l == 0 else P["xs"]
    out = P["out"] if final else P["xs"]
    nc = c.nc
    x_sb = c.sb("x_sb", [128, 16, TOK], F32, n=16)
    h_sb = c.sb("h_sb", [128, 16, TOK], BF16, n=16)
    yw_t = c.sbt("yw", [128, 16 * 512], F32)
    y_sb = [c.mkbuf(yw_t, f"yw{i}") for i in range(16)]
    hid_t = c.sbt("hid", [128, 2 * FPG * TOK], BF16)
    hidb = [c.mkbuf(hid_t, "hid0"), c.mkbuf(hid_t, "hid1")]
    gsc = [c.sb(f"gsc{i}", [128, 4, 512], F32) for i in range(2)]
    ytmp = [c.sb(f"ytmp{i}", [128, 512], F32) for i in range(2)]
    v_sb = c.sb("v_sb", [128, 56], F32)
    sq_tmp = [c.sb(f"sq{i}", [128, 512], BF16) for i in range(3)]
    rstd = c.sb("rstd", [128, 512], F32)
    sig = [c.sb(f"sig{i}", [128, 512], F32) for i in range(2)]
    wk = [c.sb(f"wk{i}", [128, 16, 128], BF16) for i in range(4)]
    ps_ssq = c.ps("ps_ssq", [128, 512])
    psA = [c.ps(f"psA{i}", [128, 512]) for i in range(6)]

    def yview(k, n=512):
        return View(y_sb[k], yw_t[:, k * 512:k * 512 + n])

    yw_bf = yw_t[:, :].bitcast(BF16)

    def wdview(slot):
        return View(y_sb[2 * slot], yw_bf[:, slot * 2048:(slot + 1) * 2048])

    hid_ap = hid_t[:, :]

    def hidview(buf, fi, tc):
        off = buf * FPG * TOK + fi * TOK + tc * 512
        return View(hidb[buf], hid_ap[:, off:off + 512])

    def gsview(k):
        return View(hidb[0], hid_ap[:, k * 512:(k + 1) * 512])

    def wgview(k, c0):
        off = FPG * TOK + k * 1024 + c0
        return View(hidb[1], hid_ap[:, off:off + 128])

    xv = xsrc.ap().rr("(k p) t -> p k t", p=128)
    for k in range(16):
        c.dma(x_sb[k][:, k, :], xv[:, k, :], q="sync")
    c.dma(v_sb[:, :], View(P["vecs"], P["vecs"].t.ap()[l]), q="sync")
    c.dma(View(hidb[1], hid_ap[:, FPG * TOK:FPG * TOK + 4096].rearrange("p (k n) -> p k n", k=4)),
          View(P["w_glu"], P["w_glu"].t.ap()[l].rearrange("(k p) n -> p k n", p=128)), q="gpsimd")

    wov = View(P["w_o"], P["w_o"].t.ap()[l].rearrange("(k p) n -> p k n", p=128))
    wgv = View(P["w_gate"], P["w_gate"].t.ap()[l].rearrange("(k p) n -> p k n", p=128))
    wuv = View(P["w_up"], P["w_up"].t.ap()[l].rearrange("(k p) n -> p k n", p=128))
    wdv = View(P["w_down"], P["w_down"].t.ap()[l])

    kblocks = [(wov, m) for m in range(16)]
    for f in range(NF):
        kblocks.append((wgv, f))
        kblocks.append((wuv, f))
    kstate = {"next": 0}

    def prefetch_k(upto):
        while kstate["next"] <= upto and kstate["next"] < len(kblocks):
            i = kstate["next"]
            src, m = kblocks[i]
            c.dma(wk[i % 4][:, :, :], src[:, :, m * 128:(m + 1) * 128], q="gpsimd")
            kstate["next"] += 1

    prefetch_k(2)
    it = 0
    for tc in range(2):
        sl = slice(tc * 512, (tc + 1) * 512)
        def load_blend(k):
            pr_, r0 = e_src_rows(k)
            yt = ytmp[k % 2]
            gsrc = g2rows(P, pr_, r0, 128)
            c.dma(yview(k), gsrc[:, tc * 512:tc * 512 + 512], q="sync")
            c.dma(yt[:, :], gsrc[:, TOK + tc * 512:TOK + tc * 512 + 512], q="sync")
            c.ts(yview(k), yview(k), m_sb[:, 0:1], ALU.mult)
            c.stt(yview(k), yt[:, :], m_sb[:, 1:2], yview(k), ALU.mult, ALU.add)

        def norm_group(k0, nk, nf):
            ks = list(range(k0, k0 + nk))
            fm_rmsnorm(c, [yview(k) for k in ks], [v_sb[:, 8 + k:9 + k] for k in ks],
                       [h_sb[k][:, k, sl] for k in ks], nf, 512, ones, eps_t, ps_ssq, sq_tmp, rstd)

        for k in range(8):
            load_blend(k)
        norm_group(0, 8, 1024)
        for k in range(4):
            pr_, r0 = e_src_rows(8 + k)
            gsrc = g2rows(P, pr_, r0, 128)
            c.dma(gsc[0][:, k, :], gsrc[:, tc * 512:tc * 512 + 512], q="sync")
            c.dma(gsc[1][:, k, :], gsrc[:, TOK + tc * 512:TOK + tc * 512 + 512], q="sync")
        for k in range(4):
            c.ts(gsc[0][:, k, :], gsc[0][:, k, :], m_sb[:, 0:1], ALU.mult)
            c.stt(gsview(k), gsc[1][:, k, :], m_sb[:, 1:2], gsc[0][:, k, :], ALU.mult, ALU.add)
        for m in range(4):
            p1 = psA[(2 * it) % 6]
            p2 = psA[(2 * it + 1) % 6]
            for k in range(4):
                c.mm(p1[:, :], wgview(k, m * 128), gsview(k), start=(k == 0), stop=(k == 3))
            for k in range(4):
                c.mm(p2[:, :], wgview(k, 512 + m * 128), gsview(k), start=(k == 0), stop=(k == 3))
            sg = sig[it % 2]
            c.act(sg[:, :], p2[:, :], AF.Sigmoid, bias=v_sb[:, 4 + m:5 + m])
            c.stt(yview(8 + m), p1[:, :], v_sb[:, m:m + 1], sg[:, :], ALU.add, ALU.mult)
            it += 1
        norm_group(8, 4, 512)
        for k in range(12, 16):
            load_blend(k)
        norm_group(12, 4, 512)

    bi = 0
    for m in range(16):
        prefetch_k(bi + 2)
        wt = wk[bi % 4]
        for tc in range(2):
            sl = slice(tc * 512, (tc + 1) * 512)
            p = psA[it % 6]
            for k in range(16):
                c.mm(p[:, :], wt[:, k, :], h_sb[k][:, k, sl], start=(k == 0), stop=(k == 15))
            c.tt(x_sb[m][:, m, sl], x_sb[m][:, m, sl], p[:, :], ALU.add)
            it += 1
        bi += 1

    for tc in range(2):
        sl = slice(tc * 512, (tc + 1) * 512)
        fm_rmsnorm(c, [x_sb[k][:, k, sl] for k in range(16)], [v_sb[:, 24 + k:25 + k] for k in range(16)],
                   [h_sb[k][:, k, sl] for k in range(16)], D, 512, ones, eps_t, ps_ssq, sq_tmp, rstd)

    def load_wd(g):
        for fi in range(FPG):
            f = g * FPG + fi
            c.dma(wdview((g % 2) * FPG + fi), wdv[f * 128:(f + 1) * 128, :], q="gpsimd")

    load_wd(0)
    for g in range(FG):
        hb = g % 2
        if g + 1 < FG:
            load_wd(g + 1)
        for fi in range(FPG):
            prefetch_k(bi + 3)
            wgt = wk[bi % 4]
            wut = wk[(bi + 1) % 4]
            for tc in range(2):
                sl = slice(tc * 512, (tc + 1) * 512)
                pg = psA[(2 * it) % 6]
                pu = psA[(2 * it + 1) % 6]
                for k in range(16):
                    c.mm(pg[:, :], wgt[:, k, :], h_sb[k][:, k, sl], start=(k == 0), stop=(k == 15))
                for k in range(16):
                    c.mm(pu[:, :], wut[:, k, :], h_sb[k][:, k, sl], start=(k == 0), stop=(k == 15))
                sg = sig[it % 2]
                c.act(sg[:, :], pg[:, :], AF.Silu)
                c.tt(hidview(hb, fi, tc), sg[:, :], pu[:, :], ALU.mult)
                it += 1
            bi += 2
        for tc in range(2):
            sl = slice(tc * 512, (tc + 1) * 512)
            for mq in range(4):
                pss = [psA[(it + j) % 6] for j in range(4)]
                for fi in range(FPG):
                    wdt = wdview((g % 2) * FPG + fi)
                    for j in range(4):
                        m = mq * 4 + j
                        c.mm(pss[j][:, :], wdt[:, m * 128:(m + 1) * 128], hidview(hb, fi, tc),
                             start=(fi == 0), stop=(fi == FPG - 1))
                for j in range(4):
                    m = mq * 4 + j
                    c.tt(x_sb[m][:, m, sl], x_sb[m][:, m, sl], pss[j][:, :], ALU.add)
                it += 4

    ov = out.ap().rr("(k p) t -> p k t", p=128)
    if final:
        for tc in range(2):
            sl = slice(tc * 512, (tc + 1) * 512)
            osb = [View(hidb[j % 2], hid_t[:, :].bitcast(F32)[:, j * 512:(j + 1) * 512]) for j in range(4)]

            def after(k, sl=sl, osb=osb):
                c.dma(ov[:, k, sl], osb[k % 4], q="sync")
            fm_rmsnorm(c, [x_sb[k][:, k, sl] for k in range(16)], [v_sb[:, 40 + k:41 + k] for k in range(16)],
                       [osb[k % 4] for k in range(16)], D, 512, ones, eps_t, ps_ssq, sq_tmp, rstd, after=after)
    else:
        for k in range(16):
            c.dma(ov[:, k, :], x_sb[k][:, k, :], q="sync")
    c.end_phase()


def build_fused(nlayers=DEPTH):
    c = Ctx()
    P = {"nlayers": nlayers}
    L = nlayers
    P["xT"] = c.dram("xT", [D, TOK], F32, "ExternalInput")
    P["w_in"] = c.dram("w_in", [L, D, IN_W], F32, "ExternalInput")
    P["gmix"] = c.dram("gmix", [L, 128, 16], F32, "ExternalInput")
    P["w_uq"] = c.dram("w_uq", [L, 512, 768], F32, "ExternalInput")
    P["w_ukv"] = c.dram("w_ukv", [L, 256, 1024], F32, "ExternalInput")
    P["gvec"] = c.dram("gvec", [L, 128, 6], F32, "ExternalInput")
    P["prm"] = c.dram("prm", [L, 128, 3, 8], F32, "ExternalInput")
    P["Bblk"] = c.dram("Bblk", [L, 128, 2, 8, 128], F32, "ExternalInput")
    P["Cblk"] = c.dram("Cblk", [L, 128, 2, 8, 128], F32, "ExternalInput")
    P["dsk"] = c.dram("dsk", [L, 128, 2], F32, "ExternalInput")
    P["w_glu"] = c.dram("w_glu", [L, 512, 1024], F32, "ExternalInput")
    P["w_o"] = c.dram("w_o", [L, D, D], F32, "ExternalInput")
    P["w_gate"] = c.dram("w_gate", [L, D, DFF], F32, "ExternalInput")
    P["w_up"] = c.dram("w_up", [L, D, DFF], F32, "ExternalInput")
    P["w_down"] = c.dram("w_down", [L, DFF, D], F32, "ExternalInput")
    P["vecs"] = c.dram("vecs", [L, 128, 56], F32, "ExternalInput")
    P["cossin"] = c.dram("cossin", [64, 2, S], F32, "ExternalInput")
    P["cmask"] = c.dram("cmask", [128, 896], BF16, "ExternalInput")
    P["dmask"] = c.dram("dmask", [128, 2432], BF16, "ExternalInput")
    P["jrow"] = c.dram("jrow", [128, 512], F32, "ExternalInput")
    identd = c.dram("ident", [128, 128], F32, "ExternalInput")
    mseld = c.dram("msel", [128, 2], F32, "ExternalInput")
    P["out"] = c.dram("outT", [D, TOK], F32, "ExternalOutput")
    P["src1"] = [c.dram(f"src1_{i}", [CR1, TOK], F32) for i in range(R1 // CR1)]
    P["G1"] = [c.dram(f"G1_{i}", [2 * CR1, TOK], F32) for i in range(R1 // CR1)]
    P["src2"] = [c.dram(f"src2_{i}", [CR2, S], F32) for i in range(R2 // CR2)]
    P["G2"] = [c.dram(f"G2_{i}", [2 * CR2, S], F32) for i in range(R2 // CR2)]
    P["xs"] = c.dram("xs", [D, TOK], F32)

    P["ones"] = c.sb("ones", [128, 128], BF16)
    P["eps_t"] = c.sb("eps_t", [128, 1], F32)
    P["ident"] = c.sb("ident_sb", [128, 128], F32)
    P["m_sb"] = c.sb("m_sb", [128, 2], F32)
    if FP32R_STATS:
        c.memset(View(P["ones"], P["ones"][:, :].ap.bitcast(F32R)), 1.0)
    else:
        c.memset(P["ones"][:, :], 1.0)
    c.memset(P["eps_t"][:, :], EPS)
    c.dma(P["ident"][:, :], identd.ap(), q="sync")
    c.dma(P["m_sb"][:, :], mseld.ap(), q="sync")

    for l in range(nlayers):
        phase_A(c, P, l)
        c.allgather(P["G1"][5], P["src1"][5])
        phase_B(c, P, l)
        c.allgather(P["G2"][0], P["src2"][0])
        c.allgather(P["G2"][1], P["src2"][1])
        phase_CD(c, P, l)
        c.allgather(P["G2"][2], P["src2"][2])
        c.allgather(P["G2"][3], P["src2"][3])
        phase_E(c, P, l)
    c.barrier()
    c.finish([P["out"]])
    return c.nc


def rope_tables():
    half = 32
    inv = (10000.0 ** (-np.arange(half, dtype=np.float32) / half)).astype(np.float32)
    ang = (np.arange(S, dtype=np.float32)[:, None] * inv[None, :]).astype(np.float32)
    cos = np.cos(ang.astype(np.float64)).astype(np.float32).T
    sin = np.sin(ang.astype(np.float64)).astype(np.float32).T
    cs = np.stack([np.concatenate([cos, cos], 0), np.concatenate([sin, sin], 0)], axis=1)
    return np.ascontiguousarray(cs)


def mask_tables():
    import ml_dtypes
    p = np.arange(128)[:, None]
    cc = np.arange(896)[None, :] - 384
    causal = ((cc - p) >= 0).astype(np.float32).astype(ml_dtypes.bfloat16)
    cc = np.arange(2432)[None, :] - 384
    dl = cc - p
    m = ((dl >= 0) & (dl <= 128)).astype(np.float32)
    m += ((dl >= 0) & (dl <= 512) & (dl % 4 == 0)).astype(np.float32)
    m += ((dl >= 0) & (dl % 16 == 0)).astype(np.float32)
    return causal, m.astype(ml_dtypes.bfloat16)


def ssm_layout(a_re, a_im, log_dt, b_re, b_im, c_re, c_im, d_skip, gh):
    g0 = gh * 16
    prm = np.zeros((128, 3, 8), np.float32)
    Bb = np.zeros((128, 2, 8, 128), np.float32)
    Cb = np.zeros((128, 2, 8, 128), np.float32)
    dsk = np.zeros((128, 2), np.float32)
    for k in range(8):
        for g2 in range(2):
            g = g0 + 2 * k + g2
            ps = slice(g2 * 64, (g2 + 1) * 64)
            prm[ps, 0, k] = a_re[g]
            prm[ps, 1, k] = a_im[g]
            prm[ps, 2, k] = log_dt[g]
            r0 = (k % 4) * 32 + g2 * 16
            Bb[r0:r0 + 16, 0, k, ps] = b_re[g].T
            Bb[r0:r0 + 16, 1, k, ps] = b_im[g].T
            Cb[ps, 0, k, r0:r0 + 16] = c_re[g].T
            Cb[ps, 1, k, r0:r0 + 16] = c_im[g].T
            dsk[r0:r0 + 16, k // 4] = d_skip[g]
    return prm, Bb, Cb, dsk


_CACHE = {}


def get_nc(name, fn):
    if name not in _CACHE:
        _CACHE[name] = fn()
    return _CACHE[name]


def run(nc, in_maps):
    res = run_bass_kernel_spmd(nc, in_maps, core_ids=list(range(NCORES)))
    return res.results


def _col(v):
    return np.ascontiguousarray(np.asarray(v, np.float32).reshape(-1, 128).T)


def make_inputs(inp, L=DEPTH):
    x = inp["x"]
    causal, dmask = mask_tables()
    cs = rope_tables()
    jrow = np.tile(np.tile(np.arange(256, dtype=np.float32), 2)[None, :], (128, 1))
    ident = np.eye(128, dtype=np.float32)
    gmix = np.stack([_col(inp["g_mix"][l]) for l in range(L)])
    gvec = np.stack([np.concatenate([_col(inp["g_q"][l]), _col(inp["g_kv"][l])], axis=1) for l in range(L)])
    vecs = np.stack([np.concatenate([_col(inp["b_glu"][l]),
                                     _col(np.concatenate([inp["g_out_mla"][l], inp["g_out_ssm"][l], inp["g_out_dil"][l]])),
                                     _col(inp["g_ffn"][l]), _col(inp["g_final"])], axis=1) for l in range(L)])
    shared = {"w_in": np.ascontiguousarray(inp["w_in"][:L]), "gmix": gmix, "gvec": gvec, "vecs": vecs,
              "w_glu": np.ascontiguousarray(inp["w_glu"][:L]), "w_o": np.ascontiguousarray(inp["w_o"][:L]),
              "w_gate": np.ascontiguousarray(inp["w_gate"][:L]), "w_up": np.ascontiguousarray(inp["w_up"][:L]),
              "w_down": np.ascontiguousarray(inp["w_down"][:L]), "cossin": cs, "cmask": causal, "dmask": dmask,
              "jrow": jrow, "ident": ident}
    half = []
    for hg in range(2):
        ss = [ssm_layout(inp["a_re"][l], inp["a_im"][l], inp["log_dt"][l], inp["b_re"][l], inp["b_im"][l],
                         inp["c_re"][l], inp["c_im"][l], inp["d_skip"][l], hg) for l in range(L)]
        msel = np.zeros((128, 2), np.float32)
        msel[:, hg] = 1.0
        half.append({"w_uq": np.ascontiguousarray(inp["w_uq"][:L, :, hg * 768:(hg + 1) * 768]),
                     "w_ukv": np.ascontiguousarray(inp["w_ukv"][:L, :, hg * 1024:(hg + 1) * 1024]),
                     "prm": np.stack([s_[0] for s_ in ss]), "Bblk": np.stack([s_[1] for s_ in ss]),
                     "Cblk": np.stack([s_[2] for s_ in ss]), "dsk": np.stack([s_[3] for s_ in ss]), "msel": msel})
    ims = []
    for b in range(NB):
        for hf in range(2):
            d = dict(shared)
            d.update(half[hf])
            d["xT"] = np.ascontiguousarray(x[b, hf * TOK:(hf + 1) * TOK].T)
            ims.append(d)
    return ims


def kernel(**inputs):
    inp = {k: np.asarray(v) for k, v in inputs.items()}
    nc = get_nc("fused", build_fused)
    res = run(nc, make_inputs(inp))
    out = np.empty((NB, S, D), np.float32)
    i = 0
    for b in range(NB):
        for hf in range(2):
            out[b, hf * TOK:(hf + 1) * TOK] = res[i]["outT"].T
            i += 1
    return out
```
